# Optimizing a Trainium2 kernel written in Bass

```python
import math
import jax, jax.numpy as jnp
from jax import lax
import numpy as np

D_MODEL = 1024
BATCH = 8
SEQ = 2048
DEPTH = 1
DEC_BATCH = 128
DEC_SEQ = 4
PAST_LEN = 16384
PAGE_SIZE = 128

N_META = 16
POOL_WIDTH = D_MODEL // 2
POOL_WINDOWS = (2, 4, 8, 16)
POOL_GROUPS = len(POOL_WINDOWS)
POOL_GROUP_DIM = POOL_WIDTH // POOL_GROUPS
POOL_BUF = max(POOL_WINDOWS) - 1
GLA_HEADS = 4
GLA_WIDTH = D_MODEL // 2
GLA_DV = GLA_WIDTH // GLA_HEADS
GLA_DK = GLA_DV // 2
GLA_KW = GLA_HEADS * GLA_DK
GLA_GATE_RANK = 16
GLA_TAU = 16.0
GLA_CHUNK = 64
D_FF = -(-8 * D_MODEL // (3 * 256)) * 256
EPS = 1e-6

_IN_SIZES = (POOL_WIDTH, GLA_KW, GLA_KW, GLA_WIDTH, GLA_WIDTH, GLA_GATE_RANK, D_MODEL, D_MODEL)
IN_DIM = sum(_IN_SIZES)
IN_SPLIT_IDX = tuple(int(s) for s in np.cumsum(_IN_SIZES)[:-1])

kernel_name = 'hybrid_pool_gla_gated_decoder_step'


def _rmsnorm(x, g):
    xf = x.astype(jnp.float32)
    y = xf * lax.rsqrt(jnp.mean(xf * xf, axis=-1, keepdims=True) + EPS)
    return (y * g.astype(jnp.float32)).astype(x.dtype)


def _pool_mix(u, buf, start_pos, w_group, scale):
    T = u.shape[1]
    ext = jnp.concatenate([buf.astype(u.dtype), u], axis=1)
    c = jnp.cumsum(ext.astype(jnp.float32), axis=1)
    c = jnp.pad(c, ((0, 0), (1, 0), (0, 0)))
    pos = start_pos + jnp.arange(T)
    uf = u.astype(jnp.float32)
    outs = []
    for gi, w in enumerate(POOL_WINDOWS):
        sl = slice(gi * POOL_GROUP_DIM, (gi + 1) * POOL_GROUP_DIM)
        hi = c[:, POOL_BUF + 1:POOL_BUF + 1 + T, sl]
        lo = c[:, POOL_BUF + 1 - w:POOL_BUF + 1 - w + T, sl]
        cnt = jnp.minimum(w, pos + 1).astype(jnp.float32)[None, :, None]
        outs.append((hi - lo) / cnt - uf[:, :, sl])
    pooled = jnp.stack(outs, axis=2).astype(u.dtype)
    mixed = jnp.einsum('btgc,gcd->btgd', pooled, w_group)
    out = mixed.reshape(u.shape) * scale
    return out, ext[:, -POOL_BUF:]


def _gla_segment(q, k, v, g, S, chunk):
    B, T, H, _ = q.shape
    n = T // chunk

    def to_chunks(a):
        return a.reshape(B, n, chunk, H, a.shape[-1]).transpose(1, 0, 3, 2, 4)

    mask = jnp.tril(jnp.ones((chunk, chunk), dtype=bool))

    def step(S, inp):
        qc, kc, vc, gc = inp
        b = jnp.cumsum(gc, axis=2)
        qe = qc * jnp.exp(b)
        ke = kc * jnp.exp(-b)
        att = jnp.where(mask, jnp.einsum('bhik,bhjk->bhij', qe, ke), 0.0)
        o = jnp.einsum('bhik,bhkv->bhiv', qe, S) + jnp.einsum('bhij,bhjv->bhiv', att, vc)
        b_last = b[:, :, -1:, :]
        S = jnp.exp(b_last[:, :, 0, :])[..., None] * S + jnp.einsum(
            'bhjk,bhjv->bhkv', kc * jnp.exp(b_last - b), vc)
        return S, o

    S, o = lax.scan(step, S, (to_chunks(q), to_chunks(k), to_chunks(v), to_chunks(g)))
    o = o.transpose(1, 0, 3, 2, 4).reshape(B, T, H, v.shape[-1])
    return o, S


def _gla(q, k, v, g, S, seg_lens):
    outs = []
    start = 0
    for L in seg_lens:
        chunk = math.gcd(L, GLA_CHUNK)
        sl = slice(start, start + L)
        o, S = _gla_segment(q[:, sl], k[:, sl], v[:, sl], g[:, sl], S, chunk)
        outs.append(o)
        start += L
    return jnp.concatenate(outs, axis=1), S


def _layer(x, pool_buf, gla_S, start_pos, seg_lens, g_mix, w_in, w_gk_up, b_gk,
           w_pool_group, pool_scale, w_pool_proj, g_gla_norm, w_gla_proj, w_out,
           g_ffn, w_ffn_in, w_ffn_out):
    B, T, _ = x.shape
    h = _rmsnorm(x, g_mix)
    u, q, k, v, og, zr, ga, gb = jnp.split(h @ w_in, IN_SPLIT_IDX, axis=-1)
    pool_out, new_buf = _pool_mix(u, pool_buf, start_pos, w_pool_group, pool_scale)
    y_a = pool_out @ w_pool_proj
    f32 = jnp.float32
    qh = q.reshape(B, T, GLA_HEADS, GLA_DK).astype(f32) * (GLA_DK ** -0.5)
    kh = k.reshape(B, T, GLA_HEADS, GLA_DK).astype(f32)
    vh = v.reshape(B, T, GLA_HEADS, GLA_DV).astype(f32)
    loga = jax.nn.log_sigmoid((zr @ w_gk_up + b_gk).astype(f32)) / GLA_TAU
    loga = loga.reshape(B, T, GLA_HEADS, GLA_DK)
    o, new_S = _gla(qh, kh, vh, loga, gla_S.astype(f32), seg_lens)
    o = o * lax.rsqrt(jnp.mean(o * o, axis=-1, keepdims=True) + EPS) * g_gla_norm.astype(f32)
    o = o * jax.nn.silu(og.reshape(B, T, GLA_HEADS, GLA_DV).astype(f32))
    y_b = o.reshape(B, T, GLA_WIDTH).astype(x.dtype) @ w_gla_proj
    merged = jax.nn.sigmoid(ga) * y_a + jax.nn.sigmoid(gb) * y_b
    x = x + merged @ w_out
    h2 = _rmsnorm(x, g_ffn)
    gate, up = jnp.split(h2 @ w_ffn_in, 2, axis=-1)
    x = x + (jax.nn.silu(gate) * up) @ w_ffn_out
    return x, new_buf, new_S.astype(x.dtype)


def setup_inputs(seed: int = 0) -> dict:
    key = jax.random.key(seed)
    ks = jax.random.split(key, 20)
    f32 = jnp.float32
    nrm = lambda k, s, sc: jax.random.normal(k, s, f32) * sc
    return {
        'x_prompt': nrm(ks[0], (BATCH, SEQ, D_MODEL), 1.0),
        'x_sample': nrm(ks[1], (DEC_BATCH, DEC_SEQ, D_MODEL), 1.0),
        'state_pool': nrm(ks[2], (DEPTH, DEC_BATCH, POOL_BUF, POOL_WIDTH), 1.0),
        'state_gla': nrm(ks[3], (DEPTH, DEC_BATCH, GLA_HEADS, GLA_DK, GLA_DV), 1.0),
        'meta_tokens': nrm(ks[4], (N_META, D_MODEL), 1.0),
        'g_mix': 1.0 + nrm(ks[5], (DEPTH, D_MODEL), 0.05),
        'w_in': nrm(ks[6], (DEPTH, D_MODEL, IN_DIM), D_MODEL ** -0.5),
        'w_gk_up': nrm(ks[7], (DEPTH, GLA_GATE_RANK, GLA_KW), GLA_GATE_RANK ** -0.5),
        'b_gk': nrm(ks[8], (DEPTH, GLA_KW), 0.01),
        'w_pool_group': nrm(ks[9], (DEPTH, POOL_GROUPS, POOL_GROUP_DIM, POOL_GROUP_DIM), POOL_GROUP_DIM ** -0.5),
        'pool_scale': 1.0 + nrm(ks[10], (DEPTH, POOL_WIDTH), 0.1),
        'w_pool_proj': nrm(ks[11], (DEPTH, POOL_WIDTH, D_MODEL), POOL_WIDTH ** -0.5),
        'g_gla_norm': 1.0 + nrm(ks[12], (DEPTH, GLA_DV), 0.05),
        'w_gla_proj': nrm(ks[13], (DEPTH, GLA_WIDTH, D_MODEL), GLA_WIDTH ** -0.5),
        'w_out': nrm(ks[14], (DEPTH, D_MODEL, D_MODEL), D_MODEL ** -0.5),
        'g_ffn': 1.0 + nrm(ks[15], (DEPTH, D_MODEL), 0.05),
        'w_ffn_in': nrm(ks[16], (DEPTH, D_MODEL, 2 * D_FF), D_MODEL ** -0.5),
        'w_ffn_out': nrm(ks[17], (DEPTH, D_FF, D_MODEL), D_FF ** -0.5),
        'g_final': 1.0 + nrm(ks[18], (D_MODEL,), 0.05),
    }


def reference(x_prompt, x_sample, state_pool, state_gla, meta_tokens, g_mix, w_in, w_gk_up,
              b_gk, w_pool_group, pool_scale, w_pool_proj, g_gla_norm, w_gla_proj, w_out,
              g_ffn, w_ffn_in, w_ffn_out, g_final):
    Bp, Tp, D = x_prompt.shape
    xp = jnp.concatenate(
        [jnp.broadcast_to(meta_tokens.astype(x_prompt.dtype)[None], (Bp, N_META, D)), x_prompt], axis=1)
    xs = x_sample
    zero_buf = jnp.zeros((Bp, POOL_BUF, POOL_WIDTH), x_prompt.dtype)
    zero_S = jnp.zeros((Bp, GLA_HEADS, GLA_DK, GLA_DV), jnp.float32)
    pool_p, gla_p, pool_s, gla_s = [], [], [], []
    for l in range(DEPTH):
        params = (g_mix[l], w_in[l], w_gk_up[l], b_gk[l], w_pool_group[l], pool_scale[l],
                  w_pool_proj[l], g_gla_norm[l], w_gla_proj[l], w_out[l], g_ffn[l],
                  w_ffn_in[l], w_ffn_out[l])
        xp, bp, sp = _layer(xp, zero_buf, zero_S, 0, (N_META, Tp), *params)
        xs, bs, ss = _layer(xs, state_pool[l], state_gla[l], PAST_LEN, (xs.shape[1],), *params)
        pool_p.append(bp)
        gla_p.append(sp)
        pool_s.append(bs)
        gla_s.append(ss)
    y_prompt = _rmsnorm(xp[:, N_META:], g_final)
    y_sample = _rmsnorm(xs, g_final)
    return (y_prompt, y_sample, jnp.stack(pool_p), jnp.stack(gla_p), jnp.stack(pool_s), jnp.stack(gla_s))
```

```python
import numpy as np
from contextlib import ExitStack
import concourse.bass as bass
import concourse.mybir as mybir
from concourse.bass_utils import run_bass_kernel_spmd

F32 = mybir.dt.float32
BF16 = mybir.dt.bfloat16
AF = mybir.ActivationFunctionType
ALU = mybir.AluOpType
AX = mybir.AxisListType

D = 1024
NH = 4
DK = 64
DV = 128
DFF = 2816
IN_DIM = 4112
EPS = 1e-6
DEBUG_STOP = None
SROWS = 96
NPT = 16
C_ID = 0
C_MU = 128
C_ML = 256
C_MA = 384
C_MUS = 896
C_MLS = 1024
C_MAS = 1152
C_SEG = 1536
C_WS = 1552
NCONST = C_WS + 4 * 3 * 96


def _build_consts():
    c = np.zeros((128, NCONST), np.float32)
    j = np.arange(128)[:, None]
    i = np.arange(128)[None, :]
    c[:, C_ID:C_ID + 128] = (j == i)
    c[:, C_MU:C_MU + 128] = np.where(j <= i, -1.0 / 16.0, 0.0)
    c[:, C_ML:C_ML + 128] = np.where(j > i, -1.0 / 16.0, 0.0)
    ma = (j <= i).astype(np.float32)
    for h in range(4):
        c[:, C_MA + h * 128:C_MA + (h + 1) * 128] = ma
    seg = np.full(128, -1)
    seg[0:16] = 0
    seg[16:32] = 1
    for s in range(16):
        seg[32 + 4 * s:36 + 4 * s] = 2 + s
    same = (seg[:, None] == seg[None, :]) & (seg[:, None] >= 0)
    js = np.arange(128)[:, None]
    is_ = np.arange(128)[None, :]
    mus = np.where(same & (js <= is_), -1.0 / 16.0, 0.0)[:, :96]
    mls = np.where(same & (js > is_), -1.0 / 16.0, 0.0)[:, :96]
    mas = (same & (js <= is_)).astype(np.float32)[:, :96]
    c[:, C_MUS:C_MUS + 96] = mus
    c[:, C_MLS:C_MLS + 96] = mls
    for h in range(4):
        c[:, C_MAS + h * 96:C_MAS + (h + 1) * 96] = mas
    for s in range(16):
        c[32 + 4 * s:36 + 4 * s, C_SEG + s] = 1.0
    for g, w in enumerate((2, 4, 8, 16)):
        wu = np.zeros((128, 96), np.float32)
        wa = np.zeros((128, 96), np.float32)
        wb = np.zeros((128, 96), np.float32)
        for t in range(16):
            cnt = min(w, t + 1)
            for jj in range(max(0, t - w + 1), t + 1):
                wu[jj, t] += 1.0 / cnt
            wu[t, t] -= 1.0
        for s in range(16):
            for t in range(4):
                col = 32 + 4 * s + t
                for tt in range(max(0, t - w + 1), t + 1):
                    wu[32 + 4 * s + tt, col] += 1.0 / w
                wu[col, col] -= 1.0
                for r in range(15):
                    if r >= 16 + t - w:
                        if s < 8:
                            wa[s * 15 + r, col] += 1.0 / w
                        else:
                            wb[(s - 8) * 15 + r, col] += 1.0 / w
        base = C_WS + g * 3 * 96
        c[:, base:base + 96] = wu
        c[:, base + 96:base + 192] = wa
        c[:, base + 192:base + 288] = wb
    return c


class StopBuild(Exception):
    pass


OP_LIMIT = None


class Trk:
    def __init__(self, nc, es):
        self.nc = nc
        self.es = es
        self.eng = {'pe': nc.tensor, 'act': nc.scalar, 'dve': nc.vector, 'pool': nc.gpsimd, 'sp': nc.sync}
        self.sems = {}
        self.tot = {}
        self.waited = {e: {} for e in self.eng}
        self.lastw = {}
        self.readers = {}
        self.n = 0
        self.tag = ''
        self.log = []
        for e in self.eng:
            self._sem('E_' + e)

    def _tick(self):
        self.n += 1
        if OP_LIMIT is not None and self.n > OP_LIMIT:
            raise StopBuild()

    def _sem(self, name):
        if name not in self.sems:
            self.sems[name] = self.es.enter_context(self.nc.semaphore(name))
            self.tot[name] = 0
        return self.sems[name]

    def _deps(self, reads, writes):
        deps = {}

        def add(ev):
            if ev is None:
                return
            s, v = ev
            if deps.get(s, 0) < v:
                deps[s] = v
        for k in reads:
            add(self.lastw.get(k))
        for k in writes:
            add(self.lastw.get(k))
            for s, v in self.readers.get(k, {}).items():
                add((s, v))
        return deps

    def _wait(self, e, deps, skip=()):
        for s, v in deps.items():
            if s in skip:
                continue
            if self.waited[e].get(s, 0) < v:
                self.eng[e].wait_ge(self.sems[s], v)
                self.waited[e][s] = v

    def _commit(self, ev, reads, writes):
        s, v = ev
        for k in reads:
            r = self.readers.setdefault(k, {})
            if r.get(s, 0) < v:
                r[s] = v
        for k in writes:
            self.lastw[k] = ev
            self.readers[k] = {}

    def op(self, e, fns, reads=(), writes=()):
        if callable(fns):
            fns = [fns]
        self._tick()
        deps = self._deps(reads, writes)
        own = 'E_' + e
        self._wait(e, deps, skip=(own,) if e == 'pe' else ())
        ins = None
        for f in fns:
            ins = f(self.eng[e])
        self.log.append((e, self.tag, len(fns)))
        self.tot[own] += 1
        ins.then_inc(self.sems[own], 1)
        self._commit((own, self.tot[own]), reads, writes)

    def dma(self, q, sem, out, in_, reads=(), writes=(), **kw):
        self._tick()
        self._sem(sem)
        deps = self._deps(reads, writes)
        self._wait(q, deps, skip=(sem,))
        ins = self.eng[q].dma_start(out=out, in_=in_, **kw)
        self.tot[sem] += 16
        ins.then_inc(self.sems[sem], 16)
        self._commit((sem, self.tot[sem]), reads, writes)

    def wait_all(self, e, semnames):
        for s in semnames:
            if self.tot.get(s, 0) > 0 and self.waited[e].get(s, 0) < self.tot[s]:
                self.eng[e].wait_ge(self.sems[s], self.tot[s])
                self.waited[e][s] = self.tot[s]


def build_program():
    nc = bass.Bass("TRN2", target_bir_lowering=False)
    es = ExitStack()

    def din(name, shape):
        return nc.dram_tensor(name, list(shape), F32, kind="ExternalInput").ap()

    def dout(name, shape):
        return nc.dram_tensor(name, list(shape), F32, kind="ExternalOutput").ap()

    xp = din("xp", [2048, D]); xs = din("xs", [64, D]); meta = din("meta", [16, D])
    spool = din("spool", [240, 512]); sgla = din("sgla", [16, 4, 64, 128])
    g_mix = din("g_mix", [1, D]); w_in = din("w_in", [D, IN_DIM]); w_gk = din("w_gk_up", [16, 256])
    b_gk = din("b_gk", [1, 256]); w_pg = din("w_pool_group", [4, 128, 128]); pscale = din("pool_scale", [512, 1])
    w_pp = din("w_pool_proj", [512, D]); g_gn = din("g_gla_norm", [1, 128]); w_gp = din("w_gla_proj", [512, D])
    w_out = din("w_out", [D, D]); g_ffn = din("g_ffn", [1, D]); w_fi = din("w_ffn_in", [D, 2 * DFF])
    w_fo = din("w_ffn_out", [DFF, D]); g_fin = din("g_final", [1, D]); cst = din("consts", [128, NCONST])
    y_p = dout("y_p", [2048, D]); y_s = dout("y_s", [64, D]); pbp = dout("pbp", [15, 512])
    gsp = dout("gsp", [4, 64, 128]); pbs = dout("pbs", [16, 15, 512]); gss = dout("gss", [16, 4, 64, 128])

    def sb(name, shape, dt=F32):
        return es.enter_context(nc.sbuf_tensor(name, list(shape), dt))

    MC = 608
    NU = 5
    xres = sb("xres", [128, NU, D])
    hT = sb("hT", [128, 8, MC], BF16)
    Eb = sb("Eb", [128, 4, 15 + 512])
    qkT = sb("qkT", [128, 4, MC])
    kTM = sb("kTM", [128, NU, 256])
    vTM = sb("vTM", [128, NU, 512], BF16)
    G2 = sb("G2", [128, NU, 512], BF16)
    spT = sb("spT", [128, NU, 256])
    zrT = sb("zrT", [16, MC])
    pooled = sb("pooled", [128, 4, 512], BF16)
    pool_out = sb("pool_out", [128, 4, MC], BF16)
    ofT = sb("ofT", [128, 4, MC], BF16)
    merged = sb("merged", [128, 8, MC], BF16)
    actb = sb("actb", [128, 8, MC], BF16)
    NRING = 4
    ring = [sb(f"ring{i}", [128, 8, 512], BF16) for i in range(NRING)]
    cs = sb("cs", [128, NCONST])
    ident_b = sb("ident_b", [128, 128], BF16)
    gmix_bc = sb("gmix_bc", [128, D]); gffn_bc = sb("gffn_bc", [128, D]); gfin_bc = sb("gfin_bc", [128, D])
    gn_bc = sb("gn_bc", [128, 512])
    pscale_c = sb("pscale_c", [128, 4])
    wpg_b = sb("wpg_b", [128, 4, 128], BF16)
    wgk_s = sb("wgk_s", [16, 256]); bgk_s = sb("bgk_s", [1, 256]); ones_r = sb("ones_r", [1, 128])
    hTM = [sb("hTM0", [128, D], BF16)]
    stat = sb("stat", [128, 96])
    Ehist = sb("Ehist", [128, 4, 15])
    sgt = [sb(f"sgt{i}", [128, 528]) for i in range(2)]
    kdecS = sb("kdecS", [128, 256])
    eT_ = [sb(f"eT{i}", [128, 2, 128]) for i in range(2)]; eiT_ = [sb(f"eiT{i}", [128, 2, 128]) for i in range(2)]
    erb_ = [sb(f"erb{i}", [128, 256]) for i in range(2)]
    qeZ_ = [sb(f"qeZ{i}", [128, 2, 2, 128], BF16) for i in range(2)]; keT_ = [sb(f"keT{i}", [128, 2, 128], BF16) for i in range(2)]
    QZ = sb("QZ", [128, 4, 96], BF16)
    kdec_ = [sb(f"kdec{i}", [128, 256], BF16) for i in range(2)]
    attm_ = [sb(f"attm{i}", [128, 4, 128], BF16) for i in range(2)]
    ofb = sb("ofb", [128, 512], BF16)
    gst_ = [sb(f"gst{i}", [128, 8]) for i in range(2)]
    Sf = sb("Sf", [128, 2, 128]); Sb_ = sb("Sb", [128, 2, 128], BF16)
    Ssb = sb("Ssb", [128, 8, 4, 128], BF16)
    Sin = [sb(f"Sin{i}", [128, 4, 128]) for i in range(2)]
    kdblk = sb("kdblk", [128, 2, 4, 2, 64], BF16)
    qeSW = sb("qeSW", [128, 2, 96], BF16); eSW = sb("eSW", [128, 2, 96])
    decS = sb("decS", [128, 8, 4])
    hist = sb("hist", [128, 2, 512])
    uTM = sb("uTM", [128, 512])
    oTs = sgt[1][:, 0:384].rearrange("p (h n) -> p h n", h=4)

    NB = 8
    banks = [es.enter_context(nc.psum_tensor(f"ps{i}", [128, 512], F32)) for i in range(NB)]
    banks_b = [b[:].bitcast(BF16) for b in banks]

    T = Trk(nc, es)
    st = {'bank': 0, 'ring': 0, 'tmp': 0}

    def nbank():
        i = st['bank']; st['bank'] = (i + 1) % NB
        return i

    def bk(i):
        return ('bank', i)

    try:
        T.op('dve', lambda e: e.memset(xres[0:SROWS, 4, :], 0.0), writes=[('x', 4)])
        T.dma('act', 'd_g1', gmix_bc[:], g_mix.partition_broadcast(128), writes=['gmix'])
        T.dma('act', 'd_cs', cs[:], cst[:, :], writes=['cs'])
        T.dma('act', 'd_c2', gffn_bc[:], g_ffn.partition_broadcast(128), writes=['gffn'])
        T.dma('act', 'd_c2', gfin_bc[:], g_fin.partition_broadcast(128), writes=['gfin'])
        for h in range(4):
            T.dma('act', 'd_c2', gn_bc[:, h * 128:(h + 1) * 128], g_gn.partition_broadcast(128), writes=['gn'])
        T.dma('act', 'd_c2', pscale_c[:], pscale.rearrange("(g p) o -> p (g o)", p=128), writes=['pscale'],
              allow_slow_non_contiguous=True)
        T.dma('act', 'd_c2', wgk_s[:], w_gk[:, :], writes=['wgk'])
        T.dma('act', 'd_c2', bgk_s[:], b_gk[:, :], writes=['bgk'])
        T.dma('pool', 'd_c3', wpg_b[:], w_pg.rearrange("g c d -> c g d"), writes=['wpg'])
        T.dma('act', 'd_c2', hist[0:120, 0, :], spool[0:120, :], writes=['hist'])
        T.dma('act', 'd_c2', hist[0:120, 1, :], spool[120:240, :], writes=['hist'])
        T.op('dve', lambda e: e.tensor_copy(out=ident_b[:], in_=cs[:, C_ID:C_ID + 128]), reads=['cs'], writes=['identb'])
        T.op('dve', lambda e: e.memset(ones_r[:], 1.0), writes=['ones'])
        T.op('dve', lambda e: e.memset(Sf[:], 0.0), writes=[('Sf', 0), ('Sf', 1), ('Sf', 2), ('Sf', 3)])
        for i_ in range(2):
            T.op('dve', lambda e: e.memset(qeZ_[i_][:], 0.0), writes=[('qeZ', i_, 0), ('qeZ', i_, 1)])
        T.op('dve', lambda e: e.memset(QZ[:], 0.0), writes=['QZ'])
        for k, ev in list(T.lastw.items()):
            if ev[0] in ('d_cs', 'd_c2', 'd_c3'):
                T.lastw[k] = (ev[0], T.tot[ev[0]])
        T.dma('sp', 'd_out', pbs[:, 0:11, :], spool.rearrange("(s r) c -> s r c", r=15)[:, 4:15, :])

        def wrows(w, r0, nr, c0, nc_):
            return w[r0:r0 + nr, c0:c0 + nc_].rearrange("(c p) n -> p c n", p=128)

        def pass_specs():
            full = lambda rg: rg[:, :, :]
            sp = []
            sp.append([(full, w_in[:, 0:512].rearrange("(c p) n -> p c n", p=128))])
            sp.append([(full, w_in[:, 512:1024].rearrange("(c p) n -> p c n", p=128))])
            sp.append([(lambda rg: rg[:, :, 0:256], w_in[:, 768:1024].rearrange("(c p) n -> p c n", p=128)),
                       (lambda rg: rg[:, :, 256:272], w_in[:, 2048:2064].rearrange("(c p) n -> p c n", p=128))])
            sp.append([(full, w_in[:, 1024:1536].rearrange("(c p) n -> p c n", p=128))])
            sp.append([(full, w_in[:, 1536:2048].rearrange("(c p) n -> p c n", p=128))])
            for half in range(2):
                sp.append([(full, w_in[:, 2064 + half * 512:2064 + (half + 1) * 512].rearrange("(c p) n -> p c n", p=128))])
                sp.append([(lambda rg: rg[:, 0:4, :], wrows(w_pp, 0, 512, half * 512, 512))])
            for half in range(2):
                sp.append([(full, w_in[:, 3088 + half * 512:3088 + (half + 1) * 512].rearrange("(c p) n -> p c n", p=128))])
                sp.append([(lambda rg: rg[:, 0:4, :], wrows(w_gp, 0, 512, half * 512, 512))])
            for half in range(2):
                sp.append([(full, wrows(w_out, 0, D, half * 512, 512))])
            for (f0, nf) in FF_GROUPS:
                for sub in range(0, nf, 4):
                    ns = min(4, nf - sub)
                    fa = f0 + sub
                    sp.append([(lambda rg, ns=ns: rg[:, :, 0:ns * 128], wrows(w_fi, 0, D, fa * 128, ns * 128))])
                    sp.append([(lambda rg, ns=ns: rg[:, :, 0:ns * 128], wrows(w_fi, 0, D, DFF + fa * 128, ns * 128))])
                for half in range(2):
                    sp.append([(lambda rg, nf=nf: rg[:, 0:nf, :], wrows(w_fo, f0 * 128, nf * 128, half * 512, 512))])
            return sp

        FF_GROUPS = [(0, 8), (8, 8), (16, 6)]
        HBK = [('hb', k_) for k_ in range(4)]
        NPASS = 4
        _ps = pass_specs()
        NSP = len(_ps)
        all_specs = []
        for _ in range(NPASS):
            all_specs += pass_specs()
        sl = {'issued': 0, 'done': [False] * len(all_specs), 'next': 0}

        def slab_topup():
            while sl['issued'] < len(all_specs):
                j = sl['issued']
                if j - NRING >= 0 and not sl['done'][j - NRING]:
                    break
                i = j % NRING
                for (vf, src) in all_specs[j]:
                    T.dma('pool', f'd_ring{i}', vf(ring[i]), src, writes=[('ring', i)])
                sl['issued'] += 1

        def next_slab():
            slab_topup()
            j = sl['next']; sl['next'] += 1
            assert j < sl['issued'], "slab not issued (missing release?)"
            return j, j % NRING

        def slab_release(j):
            sl['done'][j] = True
            slab_topup()

        def win_cols(c0, n):
            return w_in[:, c0:c0 + n].rearrange("(c p) n -> p c n", p=128)

        passes = [['S', 0, 1, 2, 3], [4, 5, 6, 7], [8, 9, 10, 11], [12, 13, 14, 15]]

        prefetched = False
        for pi, units in enumerate(passes):
            has_s = units[0] == 'S'
            pt = [u for u in units if u != 'S']
            uslot = {}
            ucol = {}
            for k, u in enumerate(pt):
                uslot[u] = k
                ucol[u] = (k * 128, 128)
            if has_s:
                uslot['S'] = 4
                ucol['S'] = (512, SROWS)
            order = (['S'] if has_s else []) + pt
            rows = {u: (SROWS if u == 'S' else 128) for u in order}
            blocks = ([('S', 512, SROWS, ['S'])] if has_s else []) + [('P', 0, 512, pt)]

            T.tag = f'p{pi}_norm1'
            actb_flat = actb[:].rearrange("p a b -> p (a b)")

            def hb_of(u):
                if u == 'S':
                    return hTM[0], ('hTM', 0)
                k = uslot[u]
                return actb_flat[:, k * 1024:(k + 1) * 1024], ('hb', k)

            def norm_a(u, gbc, gkey, ssc):
                us = uslot[u]; r = rows[u]; c0, cn = ucol[u]
                xt = xres[0:r, us, :]
                hb, hk = hb_of(u)
                T.op('act', lambda e: e.activation(out=hb[0:r, :], in_=xt, func=AF.Square, accum_out=stat[0:r, ssc:ssc + 1]),
                     reads=[('x', us)], writes=[hk, ('stat', ssc)] + (['hbX'] if hk == ('hb', 3) else []))
                T.op('act', lambda e: e.activation(out=stat[0:r, ssc + 1:ssc + 2], in_=stat[0:r, ssc:ssc + 1], func=AF.Ln,
                                                   scale=1.0 / D, bias=EPS),
                     reads=[('stat', ssc)], writes=[('stat', ssc + 1)])
                T.op('act', lambda e: e.activation(out=stat[0:r, ssc + 2:ssc + 3], in_=stat[0:r, ssc + 1:ssc + 2], func=AF.Exp,
                                                   scale=-0.5),
                     reads=[('stat', ssc + 1)], writes=[('stat', ssc + 2)])
                T.op('dve', lambda e: e.scalar_tensor_tensor(out=hb[0:r, :], in0=xt, scalar=stat[0:r, ssc + 2:ssc + 3],
                                                             in1=gbc[0:r, :], op0=ALU.mult, op1=ALU.mult),
                     reads=[('x', us), ('stat', ssc + 2), gkey], writes=[hk] + (['hbX'] if hk == ('hb', 3) else []))

            def norm_b(u):
                us = uslot[u]; r = rows[u]; c0, cn = ucol[u]
                hb, hk = hb_of(u)
                b = nbank()
                T.op('pe', [(lambda e, c=c: e.transpose(out=banks_b[b][:, c * 128:c * 128 + r], in_=hb[0:r, c * 128:(c + 1) * 128],
                                                         identity=ident_b[0:r, 0:r])) for c in range(8)],
                     reads=[hk, 'identb'], writes=[bk(b)])
                T.op('act', lambda e: e.activation(out=hT[:, :, c0:c0 + r],
                                                   in_=banks_b[b][:, :].rearrange("p (c n) -> p c n", c=8)[:, :, 0:r], func=AF.Copy),
                     reads=[bk(b)], writes=[('hT', uslot[u])])

            for u in order:
                us = uslot[u]
                if u == 'S':
                    T.dma('sp', f'd_x{us}', xres[0:16, us, :], meta[:, :], writes=[('x', us)])
                    T.dma('sp', f'd_x{us}', xres[32:96, us, :], xs[:, :], writes=[('x', us)])
                else:
                    T.dma('sp', f'd_x{us}', xres[:, us, :], xp[u * 128:(u + 1) * 128, :], writes=[('x', us)])
            if not prefetched:
                for u in order:
                    norm_a(u, gmix_bc, 'gmix', uslot[u] * 4)
                for u in order:
                    norm_b(u)

            def fm_out(slot, kcs, wcol0, blk, evac):
                _, c0, cn, bu = blk
                b = nbank()
                T.op('pe', [(lambda e, k=k: e.matmul(banks[b][:, 0:cn], lhsT=ring[slot][:, k, wcol0:wcol0 + 128],
                                                     rhs=hT[:, k, c0:c0 + cn], start=(k == 0), stop=(k == kcs - 1)))
                            for k in range(kcs)],
                     reads=[('ring', slot)] + [('hT', uslot[u]) for u in bu], writes=[bk(b)])
                evac(b)

            def tm_out(slot, u, ncols, evac, src=None, skey=None, kcs=8, wc0=0):
                src = hT if src is None else src
                skey = ('hT', uslot[u]) if skey is None else skey
                c0, cn = ucol[u]; r = rows[u]
                b = nbank()
                T.op('pe', [(lambda e, k=k: e.matmul(banks[b][0:r, 0:ncols], lhsT=src[:, k, c0:c0 + r],
                                                     rhs=ring[slot][:, k, wc0:wc0 + ncols], start=(k == 0), stop=(k == kcs - 1)))
                            for k in range(kcs)],
                     reads=[('ring', slot), skey], writes=[bk(b)])
                evac(b)

            T.tag = f'p{pi}_in_u'
            j_u, slot = next_slab()
            for blk in blocks:
                kind, c0, cn, bu = blk
                for g in range(4):
                    if kind == 'P':
                        fm_out(slot, 8, g * 128, blk, lambda b, g=g: T.op(
                            'act', lambda e: e.activation(out=Eb[:, g, 15:15 + 512], in_=banks[b][:, 0:512], func=AF.Copy),
                            reads=[bk(b)], writes=[('E', g), ('xstg', 0), ('xstg', 1)]))
                        if pi > 0:
                            T.op('act', lambda e, g=g: e.activation(out=Eb[:, g, 0:15], in_=Ehist[:, g, :], func=AF.Copy),
                                 reads=[('Ehs', g)], writes=[('Eh', g), ('xstg', 0), ('xstg', 1)])
                    else:
                        fm_out(slot, 8, g * 128, blk, lambda b, g=g: T.op(
                            'act', lambda e: e.activation(out=Eb[:, g, 0:15], in_=banks[b][:, 1:16], func=AF.Copy),
                            reads=[bk(b)], writes=[('Eh', g)]))
            if has_s:
                tm_out(slot, 'S', 512, lambda b: T.op(
                    'act', lambda e: e.activation(out=uTM[0:SROWS, :], in_=banks[b][0:SROWS, :], func=AF.Copy),
                    reads=[bk(b)], writes=['uTM']))
                for s in range(16):
                    T.dma('sp', 'd_out', pbs[s, 11:15, :], uTM[32 + 4 * s:36 + 4 * s, :], reads=['uTM'])
                for g in range(4):
                    b = nbank()
                    base = C_WS + g * 288
                    T.op('pe', [
                        lambda e: e.matmul(banks[b][:, 0:96], lhsT=uTM[0:SROWS, g * 128:(g + 1) * 128], rhs=cs[0:SROWS, base:base + 96],
                                           start=True, stop=False),
                        lambda e: e.matmul(banks[b][:, 0:96], lhsT=hist[0:120, 0, g * 128:(g + 1) * 128], rhs=cs[0:120, base + 96:base + 192],
                                           start=False, stop=False),
                        lambda e: e.matmul(banks[b][:, 0:96], lhsT=hist[0:120, 1, g * 128:(g + 1) * 128], rhs=cs[0:120, base + 192:base + 288],
                                           start=False, stop=True)],
                        reads=['uTM', 'hist', 'cs'], writes=[bk(b)])
                    T.op('dve', lambda e: e.tensor_copy(out=pooled[:, g, 0:96], in_=banks[b][:, 0:96]),
                         reads=[bk(b)], writes=[('pooled', g)])
                for g in range(4):
                    b = nbank()
                    T.op('pe', lambda e: e.matmul(banks[b][:, 0:96], lhsT=wpg_b[:, g, :], rhs=pooled[:, g, 0:96], start=True, stop=True),
                         reads=[('pooled', g), 'wpg'], writes=[bk(b)])
                    T.op('act', lambda e: e.activation(out=pool_out[:, g, 512:608], in_=banks[b][:, 0:96], func=AF.Copy,
                                                       scale=pscale_c[:, g:g + 1]),
                         reads=[bk(b), 'pscale'], writes=[('pool_out', 'S', g)])
            if pi == len(passes) - 1:
                tm_out(slot, 15, 512, lambda b: T.op(
                    'act', lambda e: e.activation(out=uTM[:, :], in_=banks[b][:, :], func=AF.Copy),
                    reads=[bk(b)], writes=['uTM']))
                T.dma('sp', 'd_out', pbp[:, :], uTM[113:128, :], reads=['uTM'])

            slab_release(j_u)
            T.tag = f'p{pi}_in_qk'
            j_qk, slot = next_slab()
            for blk in blocks:
                kind, c0, cn, bu = blk
                for j in range(4):
                    fm_out(slot, 8, j * 128, blk, lambda b, j=j: T.op(
                        'act', lambda e: e.activation(out=qkT[:, j, c0:c0 + cn], in_=banks[b][:, 0:cn], func=AF.Copy),
                        reads=[bk(b)], writes=[('qk', kind, j)]))
            T.tag = f'p{pi}_pool'
            for g in range(4):
                w = 2 << g
                ek = [('E', g), ('Eh', g)]
                T.op('dve', lambda e: e.tensor_tensor(out=sgt[0][:, 0:526], in0=Eb[:, g, 1:527], in1=Eb[:, g, 0:526], op=ALU.add),
                     reads=ek, writes=[('sgt', 0)])
                cur, ck, off, ln = sgt[0], ('sgt', 0), 14, 526
                other, ok = sgt[1], ('sgt', 1)
                sh = 2
                for lv in range(g):
                    nl = ln - sh
                    T.op('dve', lambda e, cur=cur, other=other, nl=nl, sh=sh: e.tensor_tensor(
                        out=other[:, 0:nl], in0=cur[:, sh:sh + nl], in1=cur[:, 0:nl], op=ALU.add),
                        reads=[ck], writes=[ok])
                    cur, other, ck, ok = other, cur, ok, ck
                    ln = nl
                    off -= sh
                    sh *= 2
                T.op('dve', lambda e, cur=cur, off=off: e.scalar_tensor_tensor(
                    out=pooled[:, g, :], in0=cur[:, off:off + 512], scalar=1.0 / w, in1=Eb[:, g, 15:527],
                    op0=ALU.mult, op1=ALU.subtract),
                    reads=[ck] + ek, writes=[('pooled', g)])
                T.op('act', lambda e: e.activation(out=Ehist[:, g, :], in_=Eb[:, g, 512:527], func=AF.Copy),
                     reads=[('E', g)], writes=[('Ehs', g)])

            def pool_mix(g):
                b = nbank()
                T.op('pe', lambda e: e.matmul(banks[b][:, 0:512], lhsT=wpg_b[:, g, :], rhs=pooled[:, g, :], start=True, stop=True),
                     reads=[('pooled', g), 'wpg'], writes=[bk(b)])
                T.op('act', lambda e: e.activation(out=pool_out[:, g, 0:512], in_=banks[b][:, 0:512], func=AF.Copy,
                                                   scale=pscale_c[:, g:g + 1]),
                     reads=[bk(b), 'pscale'], writes=[('pool_out', 'P', g)])

            T.tag = f'p{pi}_in_kzr'
            slab_release(j_qk)
            j_kz, slot = next_slab()
            if pi == 0:
                for c in range(8):
                    for ee in range(2):
                        T.dma('pool', 'd_ssb', Ssb[ee * 64:(ee + 1) * 64, c], sgla[2 * c + ee].rearrange("h k v -> k h v"), writes=['Ssb'])
            for blk in blocks:
                kind, c0, cn, bu = blk
                b = nbank()
                T.op('pe', [(lambda e, k=k: e.matmul(banks[b][0:16, 0:cn], lhsT=ring[slot][:, k, 256:272],
                                                     rhs=hT[:, k, c0:c0 + cn], start=(k == 0), stop=(k == 7))) for k in range(8)],
                     reads=[('ring', slot)] + [('hT', uslot[u]) for u in bu], writes=[bk(b)])
                T.op('dve', lambda e: e.tensor_copy(out=zrT[:, c0:c0 + cn], in_=banks[b][0:16, 0:cn]),
                     reads=[bk(b)], writes=[('zrT', kind)])
            for u in order:
                us = uslot[u]; r = rows[u]; c0, cn = ucol[u]
                tm_out(slot, u, 256, lambda b: T.op(
                    'dve', lambda e: e.tensor_copy(out=kTM[0:r, us, :], in_=banks[b][0:r, 0:256]),
                    reads=[bk(b)], writes=[('kTM', us)]))
                b = nbank()
                T.op('pe', [lambda e: e.matmul(banks[b][0:r, 0:256], lhsT=zrT[:, c0:c0 + r], rhs=wgk_s[:, :], start=True, stop=False),
                            lambda e: e.matmul(banks[b][0:r, 0:256], lhsT=ones_r[:, 0:r], rhs=bgk_s[:, :], start=False, stop=True)],
                     reads=[('zrT', 'S' if u == 'S' else 'P'), 'wgk', 'bgk', 'ones'], writes=[bk(b)])
                T.op('act', lambda e: e.activation(out=spT[0:r, us, :], in_=banks[b][0:r, 0:256], func=AF.Exp, scale=-1.0),
                     reads=[bk(b)], writes=[('sp', us)])
                T.op('act', lambda e: e.activation(out=spT[0:r, us, :], in_=spT[0:r, us, :], func=AF.Ln, bias=1.0),
                     reads=[('sp', us)], writes=[('sp', us)])
            T.tag = f'p{pi}_pool'
            for g in range(4):
                pool_mix(g)
            T.tag = f'p{pi}_in_vog'
            slab_release(j_kz)
            j_v, slot = next_slab()
            for u in order:
                us = uslot[u]; r = rows[u]
                tm_out(slot, u, 512, lambda b: T.op(
                    'dve', lambda e: e.tensor_copy(out=vTM[0:r, us, :], in_=banks[b][0:r, :]),
                    reads=[bk(b)], writes=[('vTM', us)]))
            slab_release(j_v)
            j_og, slot = next_slab()
            for u in order:
                us = uslot[u]; r = rows[u]
                ti = st['tmp'] % 2; st['tmp'] += 1

                def ev_og(b, us=us, r=r, ti=ti):
                    T.op('act', lambda e: e.activation(out=sgt[ti][0:r, 0:512], in_=banks[b][0:r, :], func=AF.Silu),
                         reads=[bk(b)], writes=[('sgt', ti)])
                    T.op('dve', lambda e: e.tensor_tensor(out=G2[0:r, us, :], in0=sgt[ti][0:r, 0:512], in1=gn_bc[0:r, :], op=ALU.mult),
                         reads=[('sgt', ti), 'gn'], writes=[('G2', us)])
                tm_out(slot, u, 512, ev_og)

            slab_release(j_og)
            T.tag = f'p{pi}_gla'
            deferred = []

            def gla_unit(u, bi):
                us = uslot[u]; r = rows[u]; c0, cn = ucol[u]
                eT = eT_[bi]; eiT = eiT_[bi]; erb = erb_[bi]; qeZ = qeZ_[bi]; keT = keT_[bi]; kdec = kdec_[bi]
                attm = attm_[bi]; gst = gst_[bi]
                isS = (u == 'S')
                kind = 'S' if isS else 'P'
                cmu, cml, cma = (C_MUS, C_MLS, C_MAS) if isS else (C_MU, C_ML, C_MA)
                b1 = nbank()
                T.op('pe', [
                    lambda e: e.matmul(banks[b1][:, 0:r], lhsT=spT[0:r, us, 0:128], rhs=cs[0:r, cmu:cmu + r], start=True, stop=True),
                    lambda e: e.matmul(banks[b1][:, 128:128 + r], lhsT=spT[0:r, us, 128:256], rhs=cs[0:r, cmu:cmu + r], start=True, stop=True),
                    lambda e: e.matmul(banks[b1][0:r, 256:512], lhsT=cs[0:r, cml:cml + r], rhs=spT[0:r, us, :], start=True, stop=True)],
                    reads=[('sp', us), 'cs'], writes=[bk(b1)])
                bT3 = banks[b1][:, 0:256].rearrange("p (c n) -> p c n", c=2)[:, :, 0:r]
                T.op('act', lambda e: e.activation(out=eT[:, :, 0:r], in_=bT3, func=AF.Exp), reads=[bk(b1)], writes=[('eT', bi)])
                T.op('act', lambda e: e.activation(out=eiT[:, :, 0:r], in_=bT3, func=AF.Exp, scale=-1.0), reads=[bk(b1)], writes=[('eiT', bi)])
                T.op('act', lambda e: e.activation(out=erb[0:r, :], in_=banks[b1][0:r, 256:512], func=AF.Exp), reads=[bk(b1)], writes=[('erb', bi)])
                for e_ in range(2):
                    pr = slice(e_ * 64, (e_ + 1) * 64)
                    T.op('dve', lambda e, e_=e_, pr=pr: e.scalar_tensor_tensor(
                        out=qeZ[pr, :, e_, 0:r], in0=qkT[pr, 0:2, c0:c0 + r], scalar=DK ** -0.5,
                        in1=eT[pr, :, 0:r], op0=ALU.mult, op1=ALU.mult),
                        reads=[('qk', kind, 0), ('qk', kind, 1), ('eT', bi)], writes=[('qeZ', bi, e_)])
                T.op('dve', lambda e: e.tensor_tensor(out=keT[:, :, 0:r], in0=qkT[:, 2:4, c0:c0 + r], in1=eiT[:, :, 0:r], op=ALU.mult),
                     reads=[('qk', kind, 2), ('qk', kind, 3), ('eiT', bi)], writes=[('keT', bi)])
                if isS:
                    T.op('dve', lambda e: e.tensor_tensor(out=kdecS[0:r, :], in0=kTM[0:r, us, :], in1=erb[0:r, :], op=ALU.mult),
                         reads=[('kTM', us), ('erb', bi)], writes=['kdecS'])
                    T.op('dve', lambda e: e.tensor_copy(out=kdec[0:r, :], in_=kdecS[0:r, :]), reads=['kdecS'], writes=[('kdec', bi)])
                else:
                    T.op('dve', lambda e: e.tensor_tensor(out=kdec[0:r, :], in0=kTM[0:r, us, :], in1=erb[0:r, :], op=ALU.mult),
                         reads=[('kTM', us), ('erb', bi)], writes=[('kdec', bi)])
                yield
                b2 = nbank()
                T.op('pe', [(lambda e, h=h: e.matmul(banks[b2][0:r, h * 128:h * 128 + r],
                                                     lhsT=keT[:, h // 2, 0:r], rhs=qeZ[:, h // 2, h % 2, 0:r], start=True, stop=True))
                            for h in range(4)],
                     reads=[('keT', bi), ('qeZ', bi, 0), ('qeZ', bi, 1)], writes=[bk(b2)])
                T.op('dve', lambda e: e.tensor_tensor(
                    out=attm[0:r, :, 0:r], in0=banks[b2][0:r, :].rearrange("p (h n) -> p h n", h=4)[:, :, 0:r],
                    in1=cs[0:r, cma:cma + 4 * r].rearrange("p (h n) -> p h n", h=4), op=ALU.mult),
                    reads=[bk(b2), 'cs'], writes=[('attm', bi)])
                yield
                def state_update():
                    b4 = nbank()
                    kr = 128 if not isS else 16
                    T.op('pe', [(lambda e, h=h: e.matmul(banks[b4][:, h * 128:(h + 1) * 128],
                                                         lhsT=kdec[0:kr, (h // 2) * 128:(h // 2) * 128 + 128], rhs=vTM[0:kr, us, h * 128:(h + 1) * 128],
                                                         start=True, stop=True)) for h in range(4)],
                         reads=[('kdec', bi), ('vTM', us)], writes=[bk(b4)])
                    for h in range(4):
                        pr = slice((h % 2) * 64, (h % 2) * 64 + 64)
                        if not isS:
                            T.op('dve', lambda e, h=h, pr=pr: e.scalar_tensor_tensor(
                                out=Sf[pr, h // 2, :], in0=Sf[pr, h // 2, :], scalar=eT[pr, h // 2, 127:128],
                                in1=banks[b4][pr, h * 128:(h + 1) * 128], op0=ALU.mult, op1=ALU.add),
                                reads=[('Sf', h), ('eT', bi), bk(b4)], writes=[('Sf', h)])
                        else:
                            T.op('dve', lambda e, h=h, pr=pr: e.tensor_copy(out=Sf[pr, h // 2, :], in_=banks[b4][pr, h * 128:(h + 1) * 128]),
                                 reads=[bk(b4)], writes=[('Sf', h)])
                    T.op('act', lambda e: e.activation(out=Sb_[:, :, :], in_=Sf[:, :, :], func=AF.Copy), reads=[('Sf', 0), ('Sf', 1), ('Sf', 2), ('Sf', 3)], writes=['Sb'])
                if isS:
                    state_update()
                b3 = nbank()
                if not isS:
                    fns = []
                    for h in range(4):
                        fns.append(lambda e, h=h: e.matmul(banks[b3][:, h * 128:(h + 1) * 128], lhsT=attm[:, h, :],
                                                           rhs=vTM[:, us, h * 128:(h + 1) * 128], start=True, stop=False))
                        fns.append(lambda e, h=h: e.matmul(banks[b3][:, h * 128:(h + 1) * 128], lhsT=qeZ[:, h // 2, h % 2, :],
                                                           rhs=Sb_[:, h // 2, :], start=False, stop=True))
                    T.op('pe', fns, reads=[('attm', bi), ('vTM', us), ('qeZ', bi, 0), ('qeZ', bi, 1), 'Sb'], writes=[bk(b3)])
                else:
                    T.dma('sp', 'd_swq', qeSW[64:128, :, :], qeZ[0:64, :, 0, 0:96], reads=[('qeZ', bi, 0), ('qeZ', bi, 1)], writes=['qeSW'])
                    T.dma('sp', 'd_swq', qeSW[0:64, :, :], qeZ[64:128, :, 1, 0:96], reads=[('qeZ', bi, 0), ('qeZ', bi, 1)], writes=['qeSW'])
                    T.dma('sp', 'd_swe', eSW[64:128, :, :], eT[0:64, :, 0:96], reads=[('eT', bi)], writes=['eSW'])
                    T.dma('sp', 'd_swe', eSW[0:64, :, :], eT[64:128, :, 0:96], reads=[('eT', bi)], writes=['eSW'])
                    for h in range(4):
                        for ee in range(2):
                            pr = slice(ee * 64, (ee + 1) * 64)
                            if ee == h % 2:
                                srcv = qeZ[pr, h // 2, ee, 32:96]
                                skey = ('qeZ', bi, ee)
                            else:
                                srcv = qeSW[pr, h // 2, 32:96]
                                skey = 'qeSW'
                            T.op('dve', lambda e, h=h, ee=ee, pr=pr, srcv=srcv: e.tensor_copy(
                                out=QZ[pr, h, 32:96].rearrange("p (c e t) -> p c e t", e=2, t=4)[:, :, ee, :],
                                in_=srcv.rearrange("p (c e t) -> p c e t", e=2, t=4)[:, :, ee, :]),
                                reads=[skey], writes=['QZ'])
                    bo = nbank()
                    fns = []
                    for h in range(4):
                        fns.append(lambda e, h=h: e.matmul(banks[bo][:, h * 96:(h + 1) * 96], lhsT=vTM[0:96, us, h * 128:(h + 1) * 128],
                                                           rhs=attm[0:96, h, 0:96], start=True, stop=True))
                        for c in range(8):
                            fns.append(lambda e, h=h, c=c: e.matmul(
                                banks[bo][:, h * 96 + 32 + 8 * c:h * 96 + 40 + 8 * c],
                                lhsT=Ssb[:, c, h, :], rhs=QZ[:, h, 32 + 8 * c:40 + 8 * c],
                                start=False, stop=False, skip_group_check=True))
                    T.op('pe', fns, reads=[('attm', bi), ('vTM', us), 'QZ', 'Ssb'], writes=[bk(bo)])
                    T.op('act', lambda e: e.activation(out=oTs[:, :, :], in_=banks[bo][:, 0:384].rearrange("p (h n) -> p h n", h=4),
                                                       func=AF.Copy), reads=[bk(bo)], writes=[('sgt', 1)])
                    T.op('pe', [(lambda e, h=h: e.matmul(banks[b3][0:96, h * 128:(h + 1) * 128], lhsT=oTs[:, h, :],
                                                         rhs=cs[:, C_ID:C_ID + 128], start=True, stop=True)) for h in range(4)],
                         reads=[('sgt', 1), 'cs'], writes=[bk(b3)])
                if not isS:
                    state_update()
                T.op('act', lambda e: e.activation(out=sgt[0][0:r, 0:512], in_=banks[b3][0:r, :], func=AF.Square), reads=[bk(b3)], writes=[('sgt', 0)])
                T.op('dve', lambda e: e.tensor_reduce(out=gst[0:r, 0:4], in_=sgt[0][0:r, 0:512].rearrange("p (h v) -> p h v", h=4),
                                                      op=ALU.add, axis=AX.X), reads=[('sgt', 0)], writes=[('gst0', bi)])
                T.op('act', lambda e: e.activation(out=gst[0:r, 4:8], in_=gst[0:r, 0:4], func=AF.Ln, scale=1.0 / DV, bias=EPS),
                     reads=[('gst0', bi)], writes=[('gst1', bi)])
                T.op('act', lambda e: e.activation(out=gst[0:r, 0:4], in_=gst[0:r, 4:8], func=AF.Exp, scale=-0.5),
                     reads=[('gst1', bi)], writes=[('gst0', bi)])
                for h in range(4):
                    T.op('dve', lambda e, h=h: e.scalar_tensor_tensor(
                        out=ofb[0:r, h * 128:(h + 1) * 128], in0=banks[b3][0:r, h * 128:(h + 1) * 128], scalar=gst[0:r, h:h + 1],
                        in1=G2[0:r, us, h * 128:(h + 1) * 128], op0=ALU.mult, op1=ALU.mult),
                        reads=[bk(b3), ('gst0', bi), ('G2', us)], writes=[('ofb', h)])
                yield
                b5 = nbank()
                T.op('pe', [(lambda e, h=h: e.transpose(out=banks_b[b5][:, h * 128:h * 128 + r], in_=ofb[0:r, h * 128:(h + 1) * 128],
                                                        identity=ident_b[0:r, 0:r])) for h in range(4)],
                     reads=[('ofb', 0), ('ofb', 1), ('ofb', 2), ('ofb', 3), 'identb'], writes=[bk(b5)])
                T.op('act', lambda e: e.activation(out=ofT[:, :, c0:c0 + r],
                                                   in_=banks_b[b5][:, 0:512].rearrange("p (h n) -> p h n", h=4)[:, :, 0:r], func=AF.Copy),
                     reads=[bk(b5)], writes=[('ofT', uslot[u])])
                if isS:
                    for h in range(4):
                        for ee in range(2):
                            src = eT if ee == (h % 2) else eSW
                            skey = ('eT', bi) if ee == (h % 2) else 'eSW'
                            T.op('dve', lambda e, h=h, ee=ee, src=src: e.tensor_copy(
                                out=decS[ee * 64:ee * 64 + 64, :, h], in_=src[ee * 64:ee * 64 + 64, h // 2, 35 + 4 * ee:96:8]),
                                reads=[skey], writes=['decS'])
                    def sin_load(c):
                        i2 = c % 2
                        T.dma('sp', f'd_sin{i2}', Sin[i2][0:64, :, :], sgla[2 * c].rearrange("h k v -> k h v"), writes=[('Sin', i2)])
                        T.dma('sp', f'd_sin{i2}', Sin[i2][64:128, :, :], sgla[2 * c + 1].rearrange("h k v -> k h v"), writes=[('Sin', i2)])

                    def sample_prep(c):
                        i2 = c % 2
                        for ee in range(2):
                            s_ = 2 * c + ee
                            T.op('dve', lambda e, ee=ee, s_=s_: e.tensor_scalar(out=kdblk[0:96, i2, :, ee, :], in0=kdecS[0:96, :].rearrange("p (h k) -> p h k", h=4),
                                                                                scalar1=cs[0:96, C_SEG + s_:C_SEG + s_ + 1], scalar2=None, op0=ALU.mult),
                                 reads=['kdecS', 'cs'], writes=[('kdblk', i2)])

                    def sample_pair(c, us=us):
                        bb = nbank()
                        i2 = c % 2
                        if c == 0:
                            sin_load(0)
                            sample_prep(0)
                        if c + 1 < 8:
                            sin_load(c + 1)
                        T.op('pe', [(lambda e, h=h: e.matmul(banks[bb][:, h * 128:(h + 1) * 128],
                                                             lhsT=kdblk[0:96, i2, h, :, :].rearrange("p e k -> p (e k)"),
                                                             rhs=vTM[0:96, us, h * 128:(h + 1) * 128], start=True, stop=True)) for h in range(4)],
                             reads=[('kdblk', i2), ('vTM', us)], writes=[bk(bb)])
                        for h in range(4):
                            T.op('dve', lambda e, h=h: e.scalar_tensor_tensor(
                                out=Sin[i2][:, h, :], in0=Sin[i2][:, h, :], scalar=decS[:, c, h:h + 1],
                                in1=banks[bb][:, h * 128:(h + 1) * 128], op0=ALU.mult, op1=ALU.add),
                                reads=[('Sin', i2), 'decS', bk(bb)], writes=[('Sin', i2)])
                        T.dma('sp', f'd_so{i2}', gss[2 * c].rearrange("h k v -> k h v"), Sin[i2][0:64, :, :], reads=[('Sin', i2)])
                        T.dma('sp', f'd_so{i2}', gss[2 * c + 1].rearrange("h k v -> k h v"), Sin[i2][64:128, :, :], reads=[('Sin', i2)])
                        if c + 1 < 8:
                            sample_prep(c + 1)
                    for c in range(8):
                        deferred.append(lambda c=c: sample_pair(c))
                if DEBUG_STOP == 'glaS' and isS:
                    T.wait_all('sp', [s_ for s_ in T.sems if s_.startswith('d_')])
                    T.wait_all('sp', ['E_act', 'E_dve', 'E_pe'])
                    es.close()
                    return nc
            spg_slots = {}

            def phaseA():
                for half in range(2):
                    j_ga, sga = next_slab()
                    j_pg, spg = next_slab()
                    for blk in blocks:
                        kind, c0, cn, bu = blk
                        hk = [('hT', uslot[u]) for u in bu]
                        pk = [('pool_out', kind, g) for g in range(4)]
                        for j in range(4):
                            oc = half * 4 + j
                            ba = nbank()
                            mi = 1
                            T.op('pe', [(lambda e, k=k: e.matmul(banks[ba][:, 0:cn], lhsT=ring[sga][:, k, j * 128:(j + 1) * 128],
                                                                 rhs=hT[:, k, c0:c0 + cn], start=(k == 0), stop=(k == 7))) for k in range(8)],
                                 reads=[('ring', sga)] + hk, writes=[bk(ba)])
                            yield
                            T.op('act', lambda e: e.activation(out=sgt[mi][:, 0:cn], in_=banks[ba][:, 0:cn], func=AF.Exp, scale=-1.0),
                                 reads=[bk(ba)], writes=[('sgt', mi)])
                            T.op('act', lambda e: e.activation(out=sgt[mi][:, 0:cn], in_=sgt[mi][:, 0:cn], func=AF.Ln, bias=1.0),
                                 reads=[('sgt', mi)], writes=[('sgt', mi)])
                            T.op('act', lambda e: e.activation(out=sgt[mi][:, 0:cn], in_=sgt[mi][:, 0:cn], func=AF.Exp, scale=-1.0),
                                 reads=[('sgt', mi)], writes=[('sgt', mi)])
                            bya = nbank()
                            T.op('pe', [(lambda e, k=k: e.matmul(banks[bya][:, 0:cn], lhsT=ring[spg][:, k, j * 128:(j + 1) * 128],
                                                                 rhs=pool_out[:, k, c0:c0 + cn], start=(k == 0), stop=(k == 3))) for k in range(4)],
                                 reads=[('ring', spg)] + pk, writes=[bk(bya)])
                            T.op('dve', lambda e: e.tensor_tensor(out=merged[:, oc, c0:c0 + cn], in0=banks[bya][:, 0:cn], in1=sgt[mi][:, 0:cn], op=ALU.mult),
                                 reads=[bk(bya), ('sgt', mi)], writes=[('merged', kind, oc)])
                            yield
                    slab_release(j_ga)
                    slab_release(j_pg)

            fill = phaseA()
            T.tag = f'p{pi}_gla'
            gens = [gla_unit(u, i_ % 2) for i_, u in enumerate(order)]
            nU = len(gens)

            def adv(t):
                if 0 <= t < nU:
                    T.tag = f'p{pi}_gla'
                    next(gens[t], None)
                    T.tag = f'p{pi}_mrgA'
                    next(fill, None)
            adv(0); adv(0)
            for t in range(nU):
                adv(t + 1)
                adv(t)
                adv(t + 1)
                adv(t)
            T.tag = f'p{pi}_mrgA'
            for _ in fill:
                pass
            if pi == len(passes) - 1:
                T.dma('sp', 'd_out', gsp[0:4:2].rearrange("c k v -> k c v"), Sf[0:64, :, :], reads=[('Sf', 0), ('Sf', 1), ('Sf', 2), ('Sf', 3)])
                T.dma('sp', 'd_out', gsp[1:4:2].rearrange("c k v -> k c v"), Sf[64:128, :, :], reads=[('Sf', 0), ('Sf', 1), ('Sf', 2), ('Sf', 3)])

            T.tag = f'p{pi}_mrgB'
            for half in range(2):
                j_gb, sgb = next_slab()
                j_pg, spg = next_slab()
                for blk in blocks:
                    kind, c0, cn, bu = blk
                    hk = [('hT', uslot[u]) for u in bu]
                    okk = [('ofT', uslot[u]) for u in bu]
                    for j in range(4):
                        oc = half * 4 + j
                        bb_, byb = nbank(), nbank()
                        mi = st['tmp'] % 2; st['tmp'] += 1
                        T.op('pe', [(lambda e, k=k: e.matmul(banks[bb_][:, 0:cn], lhsT=ring[sgb][:, k, j * 128:(j + 1) * 128],
                                                             rhs=hT[:, k, c0:c0 + cn], start=(k == 0), stop=(k == 7))) for k in range(8)],
                             reads=[('ring', sgb)] + hk, writes=[bk(bb_)])
                        T.op('pe', [(lambda e, k=k: e.matmul(banks[byb][:, 0:cn], lhsT=ring[spg][:, k, j * 128:(j + 1) * 128],
                                                             rhs=ofT[:, k, c0:c0 + cn], start=(k == 0), stop=(k == 3))) for k in range(4)],
                             reads=[('ring', spg)] + okk, writes=[bk(byb)])
                        T.op('act', lambda e: e.activation(out=sgt[mi][:, 0:cn], in_=banks[bb_][:, 0:cn], func=AF.Exp, scale=-1.0),
                             reads=[bk(bb_)], writes=[('sgt', mi)])
                        T.op('act', lambda e: e.activation(out=sgt[mi][:, 0:cn], in_=sgt[mi][:, 0:cn], func=AF.Ln, bias=1.0),
                             reads=[('sgt', mi)], writes=[('sgt', mi)])
                        T.op('act', lambda e: e.activation(out=sgt[mi][:, 0:cn], in_=sgt[mi][:, 0:cn], func=AF.Exp, scale=-1.0),
                             reads=[('sgt', mi)], writes=[('sgt', mi)])
                        T.op('dve', lambda e: e.tensor_tensor(out=sgt[mi][:, 0:cn], in0=banks[byb][:, 0:cn], in1=sgt[mi][:, 0:cn], op=ALU.mult),
                             reads=[bk(byb), ('sgt', mi)], writes=[('sgt', mi)])
                        T.op('dve', lambda e: e.tensor_tensor(out=merged[:, oc, c0:c0 + cn], in0=merged[:, oc, c0:c0 + cn], in1=sgt[mi][:, 0:cn], op=ALU.add),
                             reads=[('sgt', mi), ('merged', kind, oc)], writes=[('merged', kind, oc)])
                slab_release(j_gb)
                slab_release(j_pg)

            j_w0, sw0 = next_slab()
            j_w1, sw1 = next_slab()
            prev_u = None
            for u in order:
                T.tag = f'p{pi}_wout'
                us = uslot[u]; r = rows[u]
                kind = 'S' if u == 'S' else 'P'
                c0, cn = ucol[u]
                for half, slot in ((0, sw0), (1, sw1)):
                    b = nbank()
                    T.op('pe', [(lambda e, k=k: e.matmul(banks[b][0:r, :], lhsT=merged[:, k, c0:c0 + r], rhs=ring[slot][:, k, :],
                                                         start=(k == 0), stop=(k == 7))) for k in range(8)],
                         reads=[('ring', slot)] + [('merged', kind, oc) for oc in range(8)], writes=[bk(b)])
                    T.op('dve', lambda e: e.tensor_tensor(out=xres[0:r, us, half * 512:(half + 1) * 512],
                                                          in0=xres[0:r, us, half * 512:(half + 1) * 512], in1=banks[b][0:r, :], op=ALU.add),
                         reads=[bk(b), ('x', us)], writes=[('x', us)])
                T.tag = f'p{pi}_norm2'
                norm_a(u, gffn_bc, 'gffn', 20 + uslot[u] * 4)
                if prev_u is not None:
                    norm_b(prev_u)
                prev_u = u
            norm_b(prev_u)
            slab_release(j_w0)
            slab_release(j_w1)

            nxt = passes[pi + 1] if pi + 1 < len(passes) else None
            Eb_flat = Eb[:].rearrange("p g c -> p (g c)")
            EKEYS = [('E', g_) for g_ in range(4)] + [('Eh', g_) for g_ in range(4)]
            hbX = actb_flat[:, 6 * MC:6 * MC + 1024]
            HBXK = ['hbX', ('act', 'P', 6), ('act', 'P', 7), ('act', 'S', 6), ('act', 'S', 7)]

            def pf_bufs(idx):
                return (hTM[0][:, :], [('hTM', 0)]) if idx % 2 == 0 else (hbX, HBXK)

            def pf_chain(idx):
                u_ = nxt[idx]; i_ = idx % 2
                stg = Eb_flat[:, i_ * 1024:(i_ + 1) * 1024]
                ssc = 60 + idx * 4
                hb, hkeys = pf_bufs(idx)
                T.dma('sp', f'd_xs{i_}', stg, xp[u_ * 128:(u_ + 1) * 128, :], writes=[('xstg', i_)] + (EKEYS if idx == 0 else []))
                T.op('act', lambda e: e.activation(out=hb, in_=stg, func=AF.Square, accum_out=stat[:, ssc:ssc + 1]),
                     reads=[('xstg', i_)], writes=hkeys + [('stat', ssc)])
                T.op('act', lambda e: e.activation(out=stat[:, ssc + 1:ssc + 2], in_=stat[:, ssc:ssc + 1], func=AF.Ln, scale=1.0 / D, bias=EPS),
                     reads=[('stat', ssc)], writes=[('stat', ssc + 1)])
                T.op('act', lambda e: e.activation(out=stat[:, ssc + 2:ssc + 3], in_=stat[:, ssc + 1:ssc + 2], func=AF.Exp, scale=-0.5),
                     reads=[('stat', ssc + 1)], writes=[('stat', ssc + 2)])
                T.op('dve', lambda e: e.scalar_tensor_tensor(out=hb, in0=stg, scalar=stat[:, ssc + 2:ssc + 3], in1=gmix_bc[:, :],
                                                             op0=ALU.mult, op1=ALU.mult),
                     reads=[('xstg', i_), ('stat', ssc + 2), 'gmix'], writes=hkeys)

            def pf_tr(idx):
                hb, hkeys = pf_bufs(idx)
                b = nbank()
                T.op('pe', [(lambda e, c=c: e.transpose(out=banks_b[b][:, c * 128:(c + 1) * 128], in_=hb[:, c * 128:(c + 1) * 128],
                                                         identity=ident_b[:, :])) for c in range(8)],
                     reads=hkeys + ['identb'], writes=[bk(b)])
                T.op('act', lambda e: e.activation(out=hT[:, :, idx * 128:(idx + 1) * 128],
                                                   in_=banks_b[b][:, :].rearrange("p (c n) -> p c n", c=8), func=AF.Copy),
                     reads=[bk(b)], writes=[('hT', idx)])

            T.tag = f'p{pi}_ffn'
            for gi, (f0, nf) in enumerate(FF_GROUPS):
                do_pf = (nxt is not None) and gi == len(FF_GROUPS) - 1
                if do_pf:
                    T.tag = f'p{pi}_pf'
                    pf_chain(0); pf_chain(1)
                    T.tag = f'p{pi}_ffn'
                for sub in range(0, nf, 4):
                    ns = min(4, nf - sub)
                    fa = f0 + sub
                    j_sg, sg = next_slab()
                    j_su, su = next_slab()
                    s_tails = []
                    for blk in blocks:
                        kind, c0, cn, bu = blk
                        hk = [('hT', uslot[u]) for u in bu]
                        if kind == 'S':
                            nw = ns * 128
                            bg, bu_ = nbank(), nbank()
                            ti = st['tmp'] % 2; st['tmp'] += 1
                            T.op('pe', [(lambda e, k=k: e.matmul(banks[bg][0:SROWS, 0:nw], lhsT=hT[:, k, c0:c0 + SROWS], rhs=ring[sg][:, k, 0:nw],
                                                                 start=(k == 0), stop=(k == 7))) for k in range(8)],
                                 reads=[('ring', sg)] + hk, writes=[bk(bg)])
                            T.op('pe', [(lambda e, k=k: e.matmul(banks[bu_][0:SROWS, 0:nw], lhsT=hT[:, k, c0:c0 + SROWS], rhs=ring[su][:, k, 0:nw],
                                                                 start=(k == 0), stop=(k == 7))) for k in range(8)],
                                 reads=[('ring', su)] + hk, writes=[bk(bu_)])
                            T.op('act', lambda e: e.activation(out=sgt[ti][0:SROWS, 0:nw], in_=banks[bg][0:SROWS, 0:nw], func=AF.Silu),
                                 reads=[bk(bg)], writes=[('sgt', ti)])
                            okeys = [('ofb', h_) for h_ in range(4)]
                            T.op('dve', lambda e: e.tensor_tensor(out=ofb[0:SROWS, 0:nw], in0=banks[bu_][0:SROWS, 0:nw],
                                                                  in1=sgt[ti][0:SROWS, 0:nw], op=ALU.mult),
                                 reads=[bk(bu_), ('sgt', ti)], writes=okeys)
                            def s_tail(c0=c0, nw=nw, okeys=okeys, kind=kind):
                                bt = nbank()
                                T.op('pe', [(lambda e, j=j: e.transpose(out=banks_b[bt][:, j * 128:j * 128 + SROWS], in_=ofb[0:SROWS, j * 128:(j + 1) * 128],
                                                                        identity=ident_b[0:SROWS, 0:SROWS])) for j in range(ns)],
                                     reads=okeys + ['identb'], writes=[bk(bt)])
                                T.op('act', lambda e: e.activation(
                                    out=actb[:, sub:sub + ns, c0:c0 + SROWS],
                                    in_=banks_b[bt][:, 0:nw].rearrange("p (j n) -> p j n", j=ns)[:, :, 0:SROWS], func=AF.Copy),
                                    reads=[bk(bt)], writes=[('act', kind, sub + j) for j in range(ns)] + HBK)
                            s_tails.append(s_tail)
                            continue
                        for j in range(ns):
                            bg, bu_ = nbank(), nbank()
                            ti = st['tmp'] % 2; st['tmp'] += 1
                            T.op('pe', [(lambda e, k=k: e.matmul(banks[bg][:, 0:cn], lhsT=ring[sg][:, k, j * 128:(j + 1) * 128],
                                                                 rhs=hT[:, k, c0:c0 + cn], start=(k == 0), stop=(k == 7))) for k in range(8)] +
                                       [(lambda e, k=k: e.matmul(banks[bu_][:, 0:cn], lhsT=ring[su][:, k, j * 128:(j + 1) * 128],
                                                                 rhs=hT[:, k, c0:c0 + cn], start=(k == 0), stop=(k == 7))) for k in range(8)],
                                 reads=[('ring', sg), ('ring', su)] + hk, writes=[bk(bg), bk(bu_)])
                            T.op('act', lambda e: e.activation(out=sgt[ti][:, 0:cn], in_=banks[bg][:, 0:cn], func=AF.Silu),
                                 reads=[bk(bg)], writes=[('sgt', ti)])
                            T.op('dve', lambda e: e.tensor_tensor(out=actb[:, sub + j, c0:c0 + cn], in0=banks[bu_][:, 0:cn],
                                                                  in1=sgt[ti][:, 0:cn], op=ALU.mult),
                                 reads=[bk(bu_), ('sgt', ti)], writes=[('act', kind, sub + j)] + HBK)
                    for f_ in s_tails:
                        f_()
                    slab_release(j_sg)
                    slab_release(j_su)
                    if deferred:
                        T.tag = f'p{pi}_smpl'
                        deferred.pop(0)()
                        T.tag = f'p{pi}_ffn'
                if do_pf:
                    T.tag = f'p{pi}_pf'
                    pf_tr(0); pf_chain(2); pf_tr(1); pf_chain(3)
                    T.tag = f'p{pi}_ffn'
                for half in range(2):
                    if do_pf and half == 1:
                        T.tag = f'p{pi}_pf'
                        pf_tr(2); pf_tr(3)
                        T.tag = f'p{pi}_ffn'
                    j_fo, slot = next_slab()
                    for u in order:
                        us = uslot[u]; r = rows[u]
                        kind = 'S' if u == 'S' else 'P'
                        c0, cn = ucol[u]
                        b = nbank()
                        T.op('pe', [(lambda e, k=k: e.matmul(banks[b][0:r, :], lhsT=actb[:, k, c0:c0 + r], rhs=ring[slot][:, k, :],
                                                             start=(k == 0), stop=(k == nf - 1))) for k in range(nf)],
                             reads=[('ring', slot)] + [('act', kind, k) for k in range(nf)] + HBK, writes=[bk(b)])
                        T.op('dve', lambda e: e.tensor_tensor(out=xres[0:r, us, half * 512:(half + 1) * 512],
                                                              in0=xres[0:r, us, half * 512:(half + 1) * 512], in1=banks[b][0:r, :], op=ALU.add),
                             reads=[bk(b), ('x', us)], writes=[('x', us)])
                    slab_release(j_fo)
                    if deferred:
                        T.tag = f'p{pi}_smpl'
                        deferred.pop(0)()
                        T.tag = f'p{pi}_ffn'
            while deferred:
                deferred.pop(0)()
            T.tag = f'p{pi}_final'
            for u in order:
                us = uslot[u]; r = rows[u]
                ssc = 40 + us * 4
                xt = xres[0:r, us, :]
                T.op('act', lambda e: e.activation(out=hTM[0][0:r, :], in_=xt, func=AF.Square, accum_out=stat[0:r, ssc:ssc + 1]),
                     reads=[('x', us)], writes=[('hTM', 0), ('stat', ssc)])
                T.op('act', lambda e: e.activation(out=stat[0:r, ssc + 1:ssc + 2], in_=stat[0:r, ssc:ssc + 1], func=AF.Ln,
                                                   scale=1.0 / D, bias=EPS),
                     reads=[('stat', ssc)], writes=[('stat', ssc + 1)])
                T.op('act', lambda e: e.activation(out=stat[0:r, ssc + 2:ssc + 3], in_=stat[0:r, ssc + 1:ssc + 2], func=AF.Exp,
                                                   scale=-0.5),
                     reads=[('stat', ssc + 1)], writes=[('stat', ssc + 2)])
                T.op('dve', lambda e: e.scalar_tensor_tensor(out=xt, in0=xt, scalar=stat[0:r, ssc + 2:ssc + 3],
                                                             in1=gfin_bc[0:r, :], op0=ALU.mult, op1=ALU.mult),
                     reads=[('x', us), ('stat', ssc + 2), 'gfin'], writes=[('x', us)])
                if u == 'S':
                    T.dma('sp', f'd_y{us}', y_s[:, :], xres[32:96, us, :], reads=[('x', us)])
                else:
                    T.dma('sp', f'd_y{us}', y_p[u * 128:(u + 1) * 128, :], xt, reads=[('x', us)])
            prefetched = nxt is not None

    except StopBuild:
        pass
    T.wait_all('sp', [s for s in T.sems if s.startswith('d_')])
    if OP_LIMIT is not None:
        T.wait_all('sp', ['E_pe', 'E_act', 'E_dve'])
    es.close()
    global LAST_LOG
    LAST_LOG = T.log
    return nc


LAST_LOG = None
_CONSTS = None


def kernel(x_prompt, x_sample, state_pool, state_gla, meta_tokens, g_mix, w_in, w_gk_up, b_gk,
           w_pool_group, pool_scale, w_pool_proj, g_gla_norm, w_gla_proj, w_out, g_ffn,
           w_ffn_in, w_ffn_out, g_final):
    global _CONSTS
    if _CONSTS is None:
        _CONSTS = _build_consts()
    f = lambda a: np.ascontiguousarray(np.asarray(a, dtype=np.float32))
    x_prompt = f(x_prompt); x_sample = f(x_sample); state_pool = f(state_pool); state_gla = f(state_gla)
    shared = {
        "meta": f(meta_tokens), "g_mix": f(g_mix).reshape(1, D), "w_in": f(w_in)[0], "w_gk_up": f(w_gk_up)[0],
        "b_gk": f(b_gk).reshape(1, 256), "w_pool_group": f(w_pool_group)[0], "pool_scale": f(pool_scale).reshape(512, 1),
        "w_pool_proj": f(w_pool_proj)[0], "g_gla_norm": f(g_gla_norm).reshape(1, 128), "w_gla_proj": f(w_gla_proj)[0],
        "w_out": f(w_out)[0], "g_ffn": f(g_ffn).reshape(1, D), "w_ffn_in": f(w_ffn_in)[0], "w_ffn_out": f(w_ffn_out)[0],
        "g_final": f(g_final).reshape(1, D), "consts": _CONSTS,
    }
    in_maps = []
    for c in range(8):
        m = dict(shared)
        m["xp"] = x_prompt[c]
        m["xs"] = x_sample[16 * c:16 * (c + 1)].reshape(64, D)
        m["spool"] = state_pool[0, 16 * c:16 * (c + 1)].reshape(240, 512)
        m["sgla"] = state_gla[0, 16 * c:16 * (c + 1)]
        in_maps.append(m)
    nc = build_program()
    res = run_bass_kernel_spmd(nc, in_maps, core_ids=list(range(8)))
    R = res.results
    y_prompt = np.stack([R[c]["y_p"] for c in range(8)], 0).astype(np.float32)
    y_sample = np.concatenate([R[c]["y_s"].reshape(16, 4, D) for c in range(8)], 0).astype(np.float32)
    pool_p = np.stack([R[c]["pbp"] for c in range(8)], 0)[None].astype(np.float32)
    gla_p = np.stack([R[c]["gsp"] for c in range(8)], 0)[None].astype(np.float32)
    pool_s = np.concatenate([R[c]["pbs"] for c in range(8)], 0)[None].astype(np.float32)
    gla_s = np.concatenate([R[c]["gss"] for c in range(8)], 0)[None].astype(np.float32)
    return (y_prompt, y_sample, pool_p, gla_p, pool_s, gla_s)
```

```python
import numpy as np
from contextlib import ExitStack
import concourse.bass as bass
import concourse.mybir as mybir
from concourse.bass_utils import run_bass_kernel_spmd

F32 = mybir.dt.float32
BF16 = mybir.dt.bfloat16
AF = mybir.ActivationFunctionType
ALU = mybir.AluOpType
AX = mybir.AxisListType

D = 1024
NH = 4
DK = 64
DV = 128
DFF = 2816
IN_DIM = 4112
EPS = 1e-6
DEBUG_STOP = None
SROWS = 96
NPT = 16
C_ID = 0
C_MU = 128
C_ML = 256
C_MA = 384
C_MUS = 896
C_MLS = 1024
C_MAS = 1152
C_SEG = 1536
C_WS = 1552
NCONST = C_WS + 4 * 3 * 96


def _build_consts():
    c = np.zeros((128, NCONST), np.float32)
    j = np.arange(128)[:, None]
    i = np.arange(128)[None, :]
    c[:, C_ID:C_ID + 128] = (j == i)
    c[:, C_MU:C_MU + 128] = np.where(j <= i, -1.0 / 16.0, 0.0)
    c[:, C_ML:C_ML + 128] = np.where(j > i, -1.0 / 16.0, 0.0)
    ma = (j <= i).astype(np.float32)
    for h in range(4):
        c[:, C_MA + h * 128:C_MA + (h + 1) * 128] = ma
    seg = np.full(128, -1)
    seg[0:16] = 0
    seg[16:32] = 1
    for s in range(16):
        seg[32 + 4 * s:36 + 4 * s] = 2 + s
    same = (seg[:, None] == seg[None, :]) & (seg[:, None] >= 0)
    js = np.arange(128)[:, None]
    is_ = np.arange(128)[None, :]
    mus = np.where(same & (js <= is_), -1.0 / 16.0, 0.0)[:, :96]
    mls = np.where(same & (js > is_), -1.0 / 16.0, 0.0)[:, :96]
    mas = (same & (js <= is_)).astype(np.float32)[:, :96]
    c[:, C_MUS:C_MUS + 96] = mus
    c[:, C_MLS:C_MLS + 96] = mls
    for h in range(4):
        c[:, C_MAS + h * 96:C_MAS + (h + 1) * 96] = mas
    for s in range(16):
        c[32 + 4 * s:36 + 4 * s, C_SEG + s] = 1.0
    for g, w in enumerate((2, 4, 8, 16)):
        wu = np.zeros((128, 96), np.float32)
        wa = np.zeros((128, 96), np.float32)
        wb = np.zeros((128, 96), np.float32)
        for t in range(16):
            cnt = min(w, t + 1)
            for jj in range(max(0, t - w + 1), t + 1):
                wu[jj, t] += 1.0 / cnt
            wu[t, t] -= 1.0
        for s in range(16):
            for t in range(4):
                col = 32 + 4 * s + t
                for tt in range(max(0, t - w + 1), t + 1):
                    wu[32 + 4 * s + tt, col] += 1.0 / w
                wu[col, col] -= 1.0
                for r in range(15):
                    if r >= 16 + t - w:
                        if s < 8:
                            wa[s * 15 + r, col] += 1.0 / w
                        else:
                            wb[(s - 8) * 15 + r, col] += 1.0 / w
        base = C_WS + g * 3 * 96
        c[:, base:base + 96] = wu
        c[:, base + 96:base + 192] = wa
        c[:, base + 192:base + 288] = wb
    return c


class StopBuild(Exception):
    pass


OP_LIMIT = None


class Trk:
    def __init__(self, nc, es):
        self.nc = nc
        self.es = es
        self.eng = {'pe': nc.tensor, 'act': nc.scalar, 'dve': nc.vector, 'pool': nc.gpsimd, 'sp': nc.sync}
        self.sems = {}
        self.tot = {}
        self.waited = {e: {} for e in self.eng}
        self.lastw = {}
        self.readers = {}
        self.n = 0
        self.tag = ''
        self.log = []
        for e in self.eng:
            self._sem('E_' + e)

    def _tick(self):
        self.n += 1
        if OP_LIMIT is not None and self.n > OP_LIMIT:
            raise StopBuild()

    def _sem(self, name):
        if name not in self.sems:
            self.sems[name] = self.es.enter_context(self.nc.semaphore(name))
            self.tot[name] = 0
        return self.sems[name]

    def _deps(self, reads, writes):
        deps = {}

        def add(ev):
            if ev is None:
                return
            s, v = ev
            if deps.get(s, 0) < v:
                deps[s] = v
        for k in reads:
            add(self.lastw.get(k))
        for k in writes:
            add(self.lastw.get(k))
            for s, v in self.readers.get(k, {}).items():
                add((s, v))
        return deps

    def _wait(self, e, deps, skip=()):
        for s, v in deps.items():
            if s in skip:
                continue
            if self.waited[e].get(s, 0) < v:
                self.eng[e].wait_ge(self.sems[s], v)
                self.waited[e][s] = v

    def _commit(self, ev, reads, writes):
        s, v = ev
        for k in reads:
            r = self.readers.setdefault(k, {})
            if r.get(s, 0) < v:
                r[s] = v
        for k in writes:
            self.lastw[k] = ev
            self.readers[k] = {}

    def op(self, e, fns, reads=(), writes=()):
        if callable(fns):
            fns = [fns]
        self._tick()
        deps = self._deps(reads, writes)
        own = 'E_' + e
        self._wait(e, deps, skip=(own,) if e == 'pe' else ())
        ins = None
        for f in fns:
            ins = f(self.eng[e])
        self.log.append((e, self.tag, len(fns)))
        self.tot[own] += 1
        ins.then_inc(self.sems[own], 1)
        self._commit((own, self.tot[own]), reads, writes)

    def dma(self, q, sem, out, in_, reads=(), writes=(), **kw):
        self._tick()
        self._sem(sem)
        deps = self._deps(reads, writes)
        self._wait(q, deps, skip=(sem,))
        ins = self.eng[q].dma_start(out=out, in_=in_, **kw)
        self.tot[sem] += 16
        ins.then_inc(self.sems[sem], 16)
        self._commit((sem, self.tot[sem]), reads, writes)

    def wait_all(self, e, semnames):
        for s in semnames:
            if self.tot.get(s, 0) > 0 and self.waited[e].get(s, 0) < self.tot[s]:
                self.eng[e].wait_ge(self.sems[s], self.tot[s])
                self.waited[e][s] = self.tot[s]


def build_program():
    nc = bass.Bass("TRN2", target_bir_lowering=False)
    es = ExitStack()

    def din(name, shape):
        return nc.dram_tensor(name, list(shape), F32, kind="ExternalInput").ap()

    def dout(name, shape):
        return nc.dram_tensor(name, list(shape), F32, kind="ExternalOutput").ap()

    xp = din("xp", [2048, D]); xs = din("xs", [64, D]); meta = din("meta", [16, D])
    spool = din("spool", [240, 512]); sgla = din("sgla", [16, 4, 64, 128])
    g_mix = din("g_mix", [1, D]); w_in = din("w_in", [D, IN_DIM]); w_gk = din("w_gk_up", [16, 256])
    b_gk = din("b_gk", [1, 256]); w_pg = din("w_pool_group", [4, 128, 128]); pscale = din("pool_scale", [512, 1])
    w_pp = din("w_pool_proj", [512, D]); g_gn = din("g_gla_norm", [1, 128]); w_gp = din("w_gla_proj", [512, D])
    w_out = din("w_out", [D, D]); g_ffn = din("g_ffn", [1, D]); w_fi = din("w_ffn_in", [D, 2 * DFF])
    w_fo = din("w_ffn_out", [DFF, D]); g_fin = din("g_final", [1, D]); cst = din("consts", [128, NCONST])
    y_p = dout("y_p", [2048, D]); y_s = dout("y_s", [64, D]); pbp = dout("pbp", [15, 512])
    gsp = dout("gsp", [4, 64, 128]); pbs = dout("pbs", [16, 15, 512]); gss = dout("gss", [16, 4, 64, 128])

    def sb(name, shape, dt=F32):
        return es.enter_context(nc.sbuf_tensor(name, list(shape), dt))

    MC = 608
    NU = 5
    xres = sb("xres", [128, NU, D])
    hT = sb("hT", [128, 8, MC], BF16)
    Eb = sb("Eb", [128, 4, 15 + 512])
    qkT = sb("qkT", [128, 4, MC])
    kTM = sb("kTM", [128, NU, 256])
    vTM = sb("vTM", [128, NU, 512], BF16)
    G2 = sb("G2", [128, NU, 512], BF16)
    spH = sb("spH", [128, NU, 2, 256], BF16)
    zrT = sb("zrT", [16, MC])
    pooled = sb("pooled", [128, 4, 512], BF16)
    pool_out = sb("pool_out", [128, 4, MC], BF16)
    ofT = sb("ofT", [128, 4, MC], BF16)
    merged = sb("merged", [128, 8, MC], BF16)
    actb = sb("actb", [128, 8, MC], BF16)
    NRING = 4
    ring = [sb(f"ring{i}", [128, 8, 512], BF16) for i in range(NRING)]
    cs = sb("cs", [128, NCONST])
    ident_b = sb("ident_b", [128, 128], BF16)
    gmix_bc = sb("gmix_bc", [128, D]); gffn_bc = sb("gffn_bc", [128, D]); gfin_bc = sb("gfin_bc", [128, D])
    gn_bc = sb("gn_bc", [128, 512])
    pscale_c = sb("pscale_c", [128, 4])
    wpg_b = sb("wpg_b", [128, 4, 128], BF16)
    wgk_s = sb("wgk_s", [16, 256]); bgk_s = sb("bgk_s", [1, 256]); ones_r = sb("ones_r", [1, 128])
    hTM = [sb("hTM0", [128, D], BF16)]
    stat = sb("stat", [128, 96])
    Ehist = sb("Ehist", [128, 4, 15])
    sgt = [sb(f"sgt{i}", [128, 528]) for i in range(2)]
    kdecS = sb("kdecS", [128, 256])
    eT_ = [sb(f"eT{i}", [128, 2, 128]) for i in range(2)]; eiT_ = [sb(f"eiT{i}", [128, 2, 128]) for i in range(2)]
    erb_ = [sb(f"erb{i}", [128, 256]) for i in range(2)]
    qeZ_ = [sb(f"qeZ{i}", [128, 2, 2, 128], BF16) for i in range(2)]; keT_ = [sb(f"keT{i}", [128, 2, 128], BF16) for i in range(2)]
    QZ = sb("QZ", [128, 4, 96], BF16)
    kdec_ = [sb(f"kdec{i}", [128, 256], BF16) for i in range(2)]
    attm_ = [sb(f"attm{i}", [128, 4, 128], BF16) for i in range(2)]
    ofb = sb("ofb", [128, 512], BF16)
    gst_ = [sb(f"gst{i}", [128, 8]) for i in range(2)]
    Sf = sb("Sf", [128, 2, 128]); Sb_ = sb("Sb", [128, 2, 128], BF16)
    Ssb = sb("Ssb", [128, 8, 4, 128], BF16)
    Sin = [sb(f"Sin{i}", [128, 4, 128]) for i in range(2)]
    kdblk = sb("kdblk", [128, 2, 4, 2, 64], BF16)
    qeSW = sb("qeSW", [128, 2, 96], BF16); eSW = sb("eSW", [128, 2, 96])
    decS = sb("decS", [128, 8, 4])
    hist = sb("hist", [128, 2, 512])
    maskb = hist[:].bitcast(BF16)
    uTM = sb("uTM", [128, 512])
    oTs = sgt[1][:, 0:384].rearrange("p (h n) -> p h n", h=4)

    NB = 8
    banks = [es.enter_context(nc.psum_tensor(f"ps{i}", [128, 512], F32)) for i in range(NB)]
    banks_b = [b[:].bitcast(BF16) for b in banks]

    T = Trk(nc, es)
    st = {'bank': 0, 'ring': 0, 'tmp': 0}

    def nbank():
        i = st['bank']; st['bank'] = (i + 1) % NB
        return i

    def bk(i):
        return ('bank', i)

    try:
        T.op('dve', lambda e: e.memset(xres[0:SROWS, 4, :], 0.0), writes=[('x', 4)])
        T.dma('act', 'd_g1', gmix_bc[:], g_mix.partition_broadcast(128), writes=['gmix'])
        T.dma('act', 'd_cs', cs[:], cst[:, :], writes=['cs'])
        T.dma('act', 'd_c2', gffn_bc[:], g_ffn.partition_broadcast(128), writes=['gffn'])
        T.dma('act', 'd_c2', gfin_bc[:], g_fin.partition_broadcast(128), writes=['gfin'])
        for h in range(4):
            T.dma('act', 'd_c2', gn_bc[:, h * 128:(h + 1) * 128], g_gn.partition_broadcast(128), writes=['gn'])
        T.dma('act', 'd_c2', pscale_c[:], pscale.rearrange("(g p) o -> p (g o)", p=128), writes=['pscale'],
              allow_slow_non_contiguous=True)
        T.dma('act', 'd_c2', wgk_s[:], w_gk[:, :], writes=['wgk'])
        T.dma('act', 'd_c2', bgk_s[:], b_gk[:, :], writes=['bgk'])
        T.dma('pool', 'd_c3', wpg_b[:], w_pg.rearrange("g c d -> c g d"), writes=['wpg'])
        T.dma('act', 'd_c2', hist[0:120, 0, :], spool[0:120, :], writes=['hist'])
        T.dma('act', 'd_c2', hist[0:120, 1, :], spool[120:240, :], writes=['hist'])
        T.op('dve', lambda e: e.tensor_copy(out=ident_b[:], in_=cs[:, C_ID:C_ID + 128]), reads=['cs'], writes=['identb'])
        T.op('dve', lambda e: e.memset(ones_r[:], 1.0), writes=['ones'])
        T.op('dve', lambda e: e.memset(Sf[:], 0.0), writes=[('Sf', 0), ('Sf', 1), ('Sf', 2), ('Sf', 3)])
        for i_ in range(2):
            T.op('dve', lambda e: e.memset(qeZ_[i_][:], 0.0), writes=[('qeZ', i_, 0), ('qeZ', i_, 1)])
        T.op('dve', lambda e: e.memset(QZ[:], 0.0), writes=['QZ'])
        for k, ev in list(T.lastw.items()):
            if ev[0] in ('d_cs', 'd_c2', 'd_c3'):
                T.lastw[k] = (ev[0], T.tot[ev[0]])
        T.dma('sp', 'd_out', pbs[:, 0:11, :], spool.rearrange("(s r) c -> s r c", r=15)[:, 4:15, :])

        def wrows(w, r0, nr, c0, nc_):
            return w[r0:r0 + nr, c0:c0 + nc_].rearrange("(c p) n -> p c n", p=128)

        def pass_specs():
            full = lambda rg: rg[:, :, :]
            sp = []
            sp.append([(full, w_in[:, 0:512].rearrange("(c p) n -> p c n", p=128))])
            sp.append([(full, w_in[:, 512:1024].rearrange("(c p) n -> p c n", p=128))])
            sp.append([(lambda rg: rg[:, :, 0:256], w_in[:, 768:1024].rearrange("(c p) n -> p c n", p=128)),
                       (lambda rg: rg[:, :, 256:272], w_in[:, 2048:2064].rearrange("(c p) n -> p c n", p=128))])
            sp.append([(full, w_in[:, 1024:1536].rearrange("(c p) n -> p c n", p=128))])
            sp.append([(full, w_in[:, 1536:2048].rearrange("(c p) n -> p c n", p=128))])
            for half in range(2):
                sp.append([(full, w_in[:, 2064 + half * 512:2064 + (half + 1) * 512].rearrange("(c p) n -> p c n", p=128))])
                sp.append([(lambda rg: rg[:, 0:4, :], wrows(w_pp, 0, 512, half * 512, 512))])
            for half in range(2):
                sp.append([(full, w_in[:, 3088 + half * 512:3088 + (half + 1) * 512].rearrange("(c p) n -> p c n", p=128))])
                sp.append([(lambda rg: rg[:, 0:4, :], wrows(w_gp, 0, 512, half * 512, 512))])
            for half in range(2):
                sp.append([(full, wrows(w_out, 0, D, half * 512, 512))])
            for (f0, nf) in FF_GROUPS:
                for sub in range(0, nf, 4):
                    ns = min(4, nf - sub)
                    fa = f0 + sub
                    sp.append([(lambda rg, ns=ns: rg[:, :, 0:ns * 128], wrows(w_fi, 0, D, fa * 128, ns * 128))])
                    sp.append([(lambda rg, ns=ns: rg[:, :, 0:ns * 128], wrows(w_fi, 0, D, DFF + fa * 128, ns * 128))])
                for half in range(2):
                    sp.append([(lambda rg, nf=nf: rg[:, 0:nf, :], wrows(w_fo, f0 * 128, nf * 128, half * 512, 512))])
            return sp

        FF_GROUPS = [(0, 8), (8, 8), (16, 6)]
        HBK = [('hb', k_) for k_ in range(4)]
        NPASS = 4
        _ps = pass_specs()
        NSP = len(_ps)
        all_specs = []
        for _ in range(NPASS):
            all_specs += pass_specs()
        sl = {'issued': 0, 'done': [False] * len(all_specs), 'next': 0}

        def slab_topup():
            while sl['issued'] < len(all_specs):
                j = sl['issued']
                if j - NRING >= 0 and not sl['done'][j - NRING]:
                    break
                i = j % NRING
                for (vf, src) in all_specs[j]:
                    T.dma('pool', f'd_ring{i}', vf(ring[i]), src, writes=[('ring', i)])
                sl['issued'] += 1

        def next_slab():
            slab_topup()
            j = sl['next']; sl['next'] += 1
            assert j < sl['issued'], "slab not issued (missing release?)"
            return j, j % NRING

        def slab_release(j):
            sl['done'][j] = True
            slab_topup()

        def win_cols(c0, n):
            return w_in[:, c0:c0 + n].rearrange("(c p) n -> p c n", p=128)

        passes = [['S', 0, 1, 2, 3], [4, 5, 6, 7], [8, 9, 10, 11], [12, 13, 14, 15]]

        prefetched = False
        for pi, units in enumerate(passes):
            has_s = units[0] == 'S'
            pt = [u for u in units if u != 'S']
            uslot = {}
            ucol = {}
            for k, u in enumerate(pt):
                uslot[u] = k
                ucol[u] = (k * 128, 128)
            if has_s:
                uslot['S'] = 4
                ucol['S'] = (512, SROWS)
            order = (['S'] if has_s else []) + pt
            rows = {u: (SROWS if u == 'S' else 128) for u in order}
            blocks = ([('S', 512, SROWS, ['S'])] if has_s else []) + [('P', 0, 512, pt)]

            T.tag = f'p{pi}_norm1'
            actb_flat = actb[:].rearrange("p a b -> p (a b)")

            def hb_of(u):
                if u == 'S':
                    return hTM[0], ('hTM', 0)
                k = uslot[u]
                return actb_flat[:, k * 1024:(k + 1) * 1024], ('hb', k)

            def norm_a(u, gbc, gkey, ssc):
                us = uslot[u]; r = rows[u]; c0, cn = ucol[u]
                xt = xres[0:r, us, :]
                hb, hk = hb_of(u)
                T.op('act', lambda e: e.activation(out=hb[0:r, :], in_=xt, func=AF.Square, accum_out=stat[0:r, ssc:ssc + 1]),
                     reads=[('x', us)], writes=[hk, ('stat', ssc)] + (['hbX'] if hk == ('hb', 3) else []))
                T.op('act', lambda e: e.activation(out=stat[0:r, ssc + 1:ssc + 2], in_=stat[0:r, ssc:ssc + 1], func=AF.Ln,
                                                   scale=1.0 / D, bias=EPS),
                     reads=[('stat', ssc)], writes=[('stat', ssc + 1)])
                T.op('act', lambda e: e.activation(out=stat[0:r, ssc + 2:ssc + 3], in_=stat[0:r, ssc + 1:ssc + 2], func=AF.Exp,
                                                   scale=-0.5),
                     reads=[('stat', ssc + 1)], writes=[('stat', ssc + 2)])
                T.op('dve', lambda e: e.scalar_tensor_tensor(out=hb[0:r, :], in0=xt, scalar=stat[0:r, ssc + 2:ssc + 3],
                                                             in1=gbc[0:r, :], op0=ALU.mult, op1=ALU.mult),
                     reads=[('x', us), ('stat', ssc + 2), gkey], writes=[hk] + (['hbX'] if hk == ('hb', 3) else []))

            def norm_b(u):
                us = uslot[u]; r = rows[u]; c0, cn = ucol[u]
                hb, hk = hb_of(u)
                b = nbank()
                T.op('pe', [(lambda e, c=c: e.transpose(out=banks_b[b][:, c * 128:c * 128 + r], in_=hb[0:r, c * 128:(c + 1) * 128],
                                                         identity=ident_b[0:r, 0:r])) for c in range(8)],
                     reads=[hk, 'identb'], writes=[bk(b)])
                T.op('act', lambda e: e.activation(out=hT[:, :, c0:c0 + r],
                                                   in_=banks_b[b][:, :].rearrange("p (c n) -> p c n", c=8)[:, :, 0:r], func=AF.Copy),
                     reads=[bk(b)], writes=[('hT', uslot[u])])

            for u in order:
                us = uslot[u]
                if u == 'S':
                    T.dma('sp', f'd_x{us}', xres[0:16, us, :], meta[:, :], writes=[('x', us)])
                    T.dma('sp', f'd_x{us}', xres[32:96, us, :], xs[:, :], writes=[('x', us)])
                else:
                    T.dma('sp', f'd_x{us}', xres[:, us, :], xp[u * 128:(u + 1) * 128, :], writes=[('x', us)])
            if not prefetched:
                for u in order:
                    norm_a(u, gmix_bc, 'gmix', uslot[u] * 4)
                for u in order:
                    norm_b(u)

            def fm_out(slot, kcs, wcol0, blk, evac):
                _, c0, cn, bu = blk
                b = nbank()
                T.op('pe', [(lambda e, k=k: e.matmul(banks[b][:, 0:cn], lhsT=ring[slot][:, k, wcol0:wcol0 + 128],
                                                     rhs=hT[:, k, c0:c0 + cn], start=(k == 0), stop=(k == kcs - 1)))
                            for k in range(kcs)],
                     reads=[('ring', slot)] + [('hT', uslot[u]) for u in bu], writes=[bk(b)])
                evac(b)

            def tm_out(slot, u, ncols, evac, src=None, skey=None, kcs=8, wc0=0):
                src = hT if src is None else src
                skey = ('hT', uslot[u]) if skey is None else skey
                c0, cn = ucol[u]; r = rows[u]
                b = nbank()
                T.op('pe', [(lambda e, k=k: e.matmul(banks[b][0:r, 0:ncols], lhsT=src[:, k, c0:c0 + r],
                                                     rhs=ring[slot][:, k, wc0:wc0 + ncols], start=(k == 0), stop=(k == kcs - 1)))
                            for k in range(kcs)],
                     reads=[('ring', slot), skey], writes=[bk(b)])
                evac(b)

            T.tag = f'p{pi}_in_u'
            j_u, slot = next_slab()
            for blk in blocks:
                kind, c0, cn, bu = blk
                for g in range(4):
                    if kind == 'P':
                        fm_out(slot, 8, g * 128, blk, lambda b, g=g: T.op(
                            'act', lambda e: e.activation(out=Eb[:, g, 15:15 + 512], in_=banks[b][:, 0:512], func=AF.Copy),
                            reads=[bk(b)], writes=[('E', g), ('xstg', 0), ('xstg', 1)]))
                        if pi > 0:
                            T.op('act', lambda e, g=g: e.activation(out=Eb[:, g, 0:15], in_=Ehist[:, g, :], func=AF.Copy),
                                 reads=[('Ehs', g)], writes=[('Eh', g), ('xstg', 0), ('xstg', 1)])
                    else:
                        fm_out(slot, 8, g * 128, blk, lambda b, g=g: T.op(
                            'act', lambda e: e.activation(out=Eb[:, g, 0:15], in_=banks[b][:, 1:16], func=AF.Copy),
                            reads=[bk(b)], writes=[('Eh', g)]))
            if has_s:
                tm_out(slot, 'S', 512, lambda b: T.op(
                    'act', lambda e: e.activation(out=uTM[0:SROWS, :], in_=banks[b][0:SROWS, :], func=AF.Copy),
                    reads=[bk(b)], writes=['uTM']))
                for s in range(16):
                    T.dma('sp', 'd_out', pbs[s, 11:15, :], uTM[32 + 4 * s:36 + 4 * s, :], reads=['uTM'])
                for g in range(4):
                    b = nbank()
                    base = C_WS + g * 288
                    T.op('pe', [
                        lambda e: e.matmul(banks[b][:, 0:96], lhsT=uTM[0:SROWS, g * 128:(g + 1) * 128], rhs=cs[0:SROWS, base:base + 96],
                                           start=True, stop=False),
                        lambda e: e.matmul(banks[b][:, 0:96], lhsT=hist[0:120, 0, g * 128:(g + 1) * 128], rhs=cs[0:120, base + 96:base + 192],
                                           start=False, stop=False),
                        lambda e: e.matmul(banks[b][:, 0:96], lhsT=hist[0:120, 1, g * 128:(g + 1) * 128], rhs=cs[0:120, base + 192:base + 288],
                                           start=False, stop=True)],
                        reads=['uTM', 'hist', 'cs'], writes=[bk(b)])
                    T.op('dve', lambda e: e.tensor_copy(out=pooled[:, g, 0:96], in_=banks[b][:, 0:96]),
                         reads=[bk(b)], writes=[('pooled', g)])
                for mi_, (cc_, nn_) in enumerate(((C_MU, 128), (C_ML, 128), (C_MUS, 96), (C_MLS, 96))):
                    T.op('dve', lambda e, mi_=mi_, cc_=cc_, nn_=nn_: e.tensor_copy(out=maskb[:, 0, mi_ * 128:mi_ * 128 + nn_],
                                                                                   in_=cs[:, cc_:cc_ + nn_]),
                         reads=['cs'], writes=['hist', 'maskb'])
                for g in range(4):
                    b = nbank()
                    T.op('pe', lambda e: e.matmul(banks[b][:, 0:96], lhsT=wpg_b[:, g, :], rhs=pooled[:, g, 0:96], start=True, stop=True),
                         reads=[('pooled', g), 'wpg'], writes=[bk(b)])
                    T.op('act', lambda e: e.activation(out=pool_out[:, g, 512:608], in_=banks[b][:, 0:96], func=AF.Copy,
                                                       scale=pscale_c[:, g:g + 1]),
                         reads=[bk(b), 'pscale'], writes=[('pool_out', 'S', g)])
            if pi == len(passes) - 1:
                tm_out(slot, 15, 512, lambda b: T.op(
                    'act', lambda e: e.activation(out=uTM[:, :], in_=banks[b][:, :], func=AF.Copy),
                    reads=[bk(b)], writes=['uTM']))
                T.dma('sp', 'd_out', pbp[:, :], uTM[113:128, :], reads=['uTM'])

            slab_release(j_u)
            T.tag = f'p{pi}_in_qk'
            j_qk, slot = next_slab()
            for blk in blocks:
                kind, c0, cn, bu = blk
                for j in range(4):
                    fm_out(slot, 8, j * 128, blk, lambda b, j=j: T.op(
                        'act', lambda e: e.activation(out=qkT[:, j, c0:c0 + cn], in_=banks[b][:, 0:cn], func=AF.Copy),
                        reads=[bk(b)], writes=[('qk', kind, j)]))
            T.tag = f'p{pi}_pool'
            for g in range(4):
                w = 2 << g
                ek = [('E', g), ('Eh', g)]
                T.op('dve', lambda e: e.tensor_tensor(out=sgt[0][:, 0:526], in0=Eb[:, g, 1:527], in1=Eb[:, g, 0:526], op=ALU.add),
                     reads=ek, writes=[('sgt', 0)])
                cur, ck, off, ln = sgt[0], ('sgt', 0), 14, 526
                other, ok = sgt[1], ('sgt', 1)
                sh = 2
                for lv in range(g):
                    nl = ln - sh
                    T.op('dve', lambda e, cur=cur, other=other, nl=nl, sh=sh: e.tensor_tensor(
                        out=other[:, 0:nl], in0=cur[:, sh:sh + nl], in1=cur[:, 0:nl], op=ALU.add),
                        reads=[ck], writes=[ok])
                    cur, other, ck, ok = other, cur, ok, ck
                    ln = nl
                    off -= sh
                    sh *= 2
                T.op('dve', lambda e, cur=cur, off=off: e.scalar_tensor_tensor(
                    out=pooled[:, g, :], in0=cur[:, off:off + 512], scalar=1.0 / w, in1=Eb[:, g, 15:527],
                    op0=ALU.mult, op1=ALU.subtract),
                    reads=[ck] + ek, writes=[('pooled', g)])
                T.op('act', lambda e: e.activation(out=Ehist[:, g, :], in_=Eb[:, g, 512:527], func=AF.Copy),
                     reads=[('E', g)], writes=[('Ehs', g)])

            def pool_mix(g):
                b = nbank()
                T.op('pe', lambda e: e.matmul(banks[b][:, 0:512], lhsT=wpg_b[:, g, :], rhs=pooled[:, g, :], start=True, stop=True),
                     reads=[('pooled', g), 'wpg'], writes=[bk(b)])
                T.op('act', lambda e: e.activation(out=pool_out[:, g, 0:512], in_=banks[b][:, 0:512], func=AF.Copy,
                                                   scale=pscale_c[:, g:g + 1]),
                     reads=[bk(b), 'pscale'], writes=[('pool_out', 'P', g)])

            T.tag = f'p{pi}_in_kzr'
            slab_release(j_qk)
            j_kz, slot = next_slab()
            if pi == 0:
                for c in range(8):
                    for ee in range(2):
                        T.dma('pool', 'd_ssb', Ssb[ee * 64:(ee + 1) * 64, c], sgla[2 * c + ee].rearrange("h k v -> k h v"), writes=['Ssb'])
            for blk in blocks:
                kind, c0, cn, bu = blk
                b = nbank()
                T.op('pe', [(lambda e, k=k: e.matmul(banks[b][0:16, 0:cn], lhsT=ring[slot][:, k, 256:272],
                                                     rhs=hT[:, k, c0:c0 + cn], start=(k == 0), stop=(k == 7))) for k in range(8)],
                     reads=[('ring', slot)] + [('hT', uslot[u]) for u in bu], writes=[bk(b)])
                T.op('dve', lambda e: e.tensor_copy(out=zrT[:, c0:c0 + cn], in_=banks[b][0:16, 0:cn]),
                     reads=[bk(b)], writes=[('zrT', kind)])
            for u in order:
                us = uslot[u]; r = rows[u]; c0, cn = ucol[u]
                tm_out(slot, u, 256, lambda b: T.op(
                    'dve', lambda e: e.tensor_copy(out=kTM[0:r, us, :], in_=banks[b][0:r, 0:256]),
                    reads=[bk(b)], writes=[('kTM', us)]))
                b = nbank()
                T.op('pe', [lambda e: e.matmul(banks[b][0:r, 0:256], lhsT=zrT[:, c0:c0 + r], rhs=wgk_s[:, :], start=True, stop=False),
                            lambda e: e.matmul(banks[b][0:r, 0:256], lhsT=ones_r[:, 0:r], rhs=bgk_s[:, :], start=False, stop=True)],
                     reads=[('zrT', 'S' if u == 'S' else 'P'), 'wgk', 'bgk', 'ones'], writes=[bk(b)])
                ti = st['tmp'] % 2; st['tmp'] += 1
                T.op('act', lambda e: e.activation(out=sgt[ti][0:r, 0:256], in_=banks[b][0:r, 0:256], func=AF.Exp, scale=-1.0),
                     reads=[bk(b)], writes=[('sgt', ti)])
                T.op('act', lambda e: e.activation(out=sgt[ti][0:r, 0:256], in_=sgt[ti][0:r, 0:256], func=AF.Ln, bias=1.0),
                     reads=[('sgt', ti)], writes=[('sgt', ti)])
                T.op('dve', lambda e: e.tensor_copy(out=spH[0:r, us, 0, :], in_=sgt[ti][0:r, 0:256]),
                     reads=[('sgt', ti)], writes=[('sp', us)])
                T.op('dve', lambda e: e.tensor_tensor(out=spH[0:r, us, 1, :], in0=sgt[ti][0:r, 0:256], in1=spH[0:r, us, 0, :], op=ALU.subtract),
                     reads=[('sgt', ti), ('sp', us)], writes=[('sp', us)])
            T.tag = f'p{pi}_pool'
            for g in range(4):
                pool_mix(g)
            T.tag = f'p{pi}_in_vog'
            slab_release(j_kz)
            j_v, slot = next_slab()
            for u in order:
                us = uslot[u]; r = rows[u]
                tm_out(slot, u, 512, lambda b: T.op(
                    'dve', lambda e: e.tensor_copy(out=vTM[0:r, us, :], in_=banks[b][0:r, :]),
                    reads=[bk(b)], writes=[('vTM', us)]))
            slab_release(j_v)
            j_og, slot = next_slab()
            for u in order:
                us = uslot[u]; r = rows[u]
                ti = st['tmp'] % 2; st['tmp'] += 1

                def ev_og(b, us=us, r=r, ti=ti):
                    T.op('act', lambda e: e.activation(out=sgt[ti][0:r, 0:512], in_=banks[b][0:r, :], func=AF.Silu),
                         reads=[bk(b)], writes=[('sgt', ti)])
                    T.op('dve', lambda e: e.tensor_tensor(out=G2[0:r, us, :], in0=sgt[ti][0:r, 0:512], in1=gn_bc[0:r, :], op=ALU.mult),
                         reads=[('sgt', ti), 'gn'], writes=[('G2', us)])
                tm_out(slot, u, 512, ev_og)

            slab_release(j_og)
            T.tag = f'p{pi}_gla'
            deferred = []

            def gla_unit(u, bi):
                us = uslot[u]; r = rows[u]; c0, cn = ucol[u]
                eT = eT_[bi]; eiT = eiT_[bi]; erb = erb_[bi]; qeZ = qeZ_[bi]; keT = keT_[bi]; kdec = kdec_[bi]
                attm = attm_[bi]; gst = gst_[bi]
                isS = (u == 'S')
                kind = 'S' if isS else 'P'
                cmu, cml, cma = (C_MUS, C_MLS, C_MAS) if isS else (C_MU, C_ML, C_MA)
                b1 = nbank()
                mub = maskb[0:r, 0, (256 if isS else 0):(256 if isS else 0) + r]
                mlb = maskb[0:r, 0, (384 if isS else 128):(384 if isS else 128) + r]
                T.op('pe', [
                    lambda e: e.matmul(banks[b1][:, 0:r], lhsT=spH[0:r, us, 0, 0:128], rhs=mub, start=True, stop=False),
                    lambda e: e.matmul(banks[b1][:, 0:r], lhsT=spH[0:r, us, 1, 0:128], rhs=mub, start=False, stop=True),
                    lambda e: e.matmul(banks[b1][:, 128:128 + r], lhsT=spH[0:r, us, 0, 128:256], rhs=mub, start=True, stop=False),
                    lambda e: e.matmul(banks[b1][:, 128:128 + r], lhsT=spH[0:r, us, 1, 128:256], rhs=mub, start=False, stop=True),
                    lambda e: e.matmul(banks[b1][0:r, 256:512], lhsT=mlb, rhs=spH[0:r, us, 0, :], start=True, stop=False),
                    lambda e: e.matmul(banks[b1][0:r, 256:512], lhsT=mlb, rhs=spH[0:r, us, 1, :], start=False, stop=True)],
                    reads=[('sp', us), 'maskb'], writes=[bk(b1)])
                bT3 = banks[b1][:, 0:256].rearrange("p (c n) -> p c n", c=2)[:, :, 0:r]
                T.op('act', lambda e: e.activation(out=eT[:, :, 0:r], in_=bT3, func=AF.Exp), reads=[bk(b1)], writes=[('eT', bi)])
                T.op('act', lambda e: e.activation(out=eiT[:, :, 0:r], in_=bT3, func=AF.Exp, scale=-1.0), reads=[bk(b1)], writes=[('eiT', bi)])
                T.op('act', lambda e: e.activation(out=erb[0:r, :], in_=banks[b1][0:r, 256:512], func=AF.Exp), reads=[bk(b1)], writes=[('erb', bi)])
                for e_ in range(2):
                    pr = slice(e_ * 64, (e_ + 1) * 64)
                    T.op('dve', lambda e, e_=e_, pr=pr: e.scalar_tensor_tensor(
                        out=qeZ[pr, :, e_, 0:r], in0=qkT[pr, 0:2, c0:c0 + r], scalar=DK ** -0.5,
                        in1=eT[pr, :, 0:r], op0=ALU.mult, op1=ALU.mult),
                        reads=[('qk', kind, 0), ('qk', kind, 1), ('eT', bi)], writes=[('qeZ', bi, e_)])
                T.op('dve', lambda e: e.tensor_tensor(out=keT[:, :, 0:r], in0=qkT[:, 2:4, c0:c0 + r], in1=eiT[:, :, 0:r], op=ALU.mult),
                     reads=[('qk', kind, 2), ('qk', kind, 3), ('eiT', bi)], writes=[('keT', bi)])
                if isS:
                    T.op('dve', lambda e: e.tensor_tensor(out=kdecS[0:r, :], in0=kTM[0:r, us, :], in1=erb[0:r, :], op=ALU.mult),
                         reads=[('kTM', us), ('erb', bi)], writes=['kdecS'])
                    T.op('dve', lambda e: e.tensor_copy(out=kdec[0:r, :], in_=kdecS[0:r, :]), reads=['kdecS'], writes=[('kdec', bi)])
                else:
                    T.op('dve', lambda e: e.tensor_tensor(out=kdec[0:r, :], in0=kTM[0:r, us, :], in1=erb[0:r, :], op=ALU.mult),
                         reads=[('kTM', us), ('erb', bi)], writes=[('kdec', bi)])
                yield
                b2 = nbank()
                T.op('pe', [(lambda e, h=h: e.matmul(banks[b2][0:r, h * 128:h * 128 + r],
                                                     lhsT=keT[:, h // 2, 0:r], rhs=qeZ[:, h // 2, h % 2, 0:r], start=True, stop=True))
                            for h in range(4)],
                     reads=[('keT', bi), ('qeZ', bi, 0), ('qeZ', bi, 1)], writes=[bk(b2)])
                T.op('dve', lambda e: e.tensor_tensor(
                    out=attm[0:r, :, 0:r], in0=banks[b2][0:r, :].rearrange("p (h n) -> p h n", h=4)[:, :, 0:r],
                    in1=cs[0:r, cma:cma + 4 * r].rearrange("p (h n) -> p h n", h=4), op=ALU.mult),
                    reads=[bk(b2), 'cs'], writes=[('attm', bi)])
                yield
                def state_update():
                    b4 = nbank()
                    kr = 128 if not isS else 16
                    T.op('pe', [(lambda e, h=h: e.matmul(banks[b4][:, h * 128:(h + 1) * 128],
                                                         lhsT=kdec[0:kr, (h // 2) * 128:(h // 2) * 128 + 128], rhs=vTM[0:kr, us, h * 128:(h + 1) * 128],
                                                         start=True, stop=True)) for h in range(4)],
                         reads=[('kdec', bi), ('vTM', us)], writes=[bk(b4)])
                    for h in range(4):
                        pr = slice((h % 2) * 64, (h % 2) * 64 + 64)
                        if not isS:
                            T.op('dve', lambda e, h=h, pr=pr: e.scalar_tensor_tensor(
                                out=Sf[pr, h // 2, :], in0=Sf[pr, h // 2, :], scalar=eT[pr, h // 2, 127:128],
                                in1=banks[b4][pr, h * 128:(h + 1) * 128], op0=ALU.mult, op1=ALU.add),
                                reads=[('Sf', h), ('eT', bi), bk(b4)], writes=[('Sf', h)])
                        else:
                            T.op('dve', lambda e, h=h, pr=pr: e.tensor_copy(out=Sf[pr, h // 2, :], in_=banks[b4][pr, h * 128:(h + 1) * 128]),
                                 reads=[bk(b4)], writes=[('Sf', h)])
                    T.op('act', lambda e: e.activation(out=Sb_[:, :, :], in_=Sf[:, :, :], func=AF.Copy), reads=[('Sf', 0), ('Sf', 1), ('Sf', 2), ('Sf', 3)], writes=['Sb'])
                if isS:
                    state_update()
                b3 = nbank()
                if not isS:
                    fns = []
                    for h in range(4):
                        fns.append(lambda e, h=h: e.matmul(banks[b3][:, h * 128:(h + 1) * 128], lhsT=attm[:, h, :],
                                                           rhs=vTM[:, us, h * 128:(h + 1) * 128], start=True, stop=False))
                        fns.append(lambda e, h=h: e.matmul(banks[b3][:, h * 128:(h + 1) * 128], lhsT=qeZ[:, h // 2, h % 2, :],
                                                           rhs=Sb_[:, h // 2, :], start=False, stop=True))
                    T.op('pe', fns, reads=[('attm', bi), ('vTM', us), ('qeZ', bi, 0), ('qeZ', bi, 1), 'Sb'], writes=[bk(b3)])
                else:
                    T.dma('sp', 'd_swq', qeSW[64:128, :, :], qeZ[0:64, :, 0, 0:96], reads=[('qeZ', bi, 0), ('qeZ', bi, 1)], writes=['qeSW'])
                    T.dma('sp', 'd_swq', qeSW[0:64, :, :], qeZ[64:128, :, 1, 0:96], reads=[('qeZ', bi, 0), ('qeZ', bi, 1)], writes=['qeSW'])
                    T.dma('sp', 'd_swe', eSW[64:128, :, :], eT[0:64, :, 0:96], reads=[('eT', bi)], writes=['eSW'])
                    T.dma('sp', 'd_swe', eSW[0:64, :, :], eT[64:128, :, 0:96], reads=[('eT', bi)], writes=['eSW'])
                    for h in range(4):
                        for ee in range(2):
                            pr = slice(ee * 64, (ee + 1) * 64)
                            if ee == h % 2:
                                srcv = qeZ[pr, h // 2, ee, 32:96]
                                skey = ('qeZ', bi, ee)
                            else:
                                srcv = qeSW[pr, h // 2, 32:96]
                                skey = 'qeSW'
                            T.op('dve', lambda e, h=h, ee=ee, pr=pr, srcv=srcv: e.tensor_copy(
                                out=QZ[pr, h, 32:96].rearrange("p (c e t) -> p c e t", e=2, t=4)[:, :, ee, :],
                                in_=srcv.rearrange("p (c e t) -> p c e t", e=2, t=4)[:, :, ee, :]),
                                reads=[skey], writes=['QZ'])
                    bo = nbank()
                    fns = []
                    for h in range(4):
                        fns.append(lambda e, h=h: e.matmul(banks[bo][:, h * 96:(h + 1) * 96], lhsT=vTM[0:96, us, h * 128:(h + 1) * 128],
                                                           rhs=attm[0:96, h, 0:96], start=True, stop=True))
                        for c in range(8):
                            fns.append(lambda e, h=h, c=c: e.matmul(
                                banks[bo][:, h * 96 + 32 + 8 * c:h * 96 + 40 + 8 * c],
                                lhsT=Ssb[:, c, h, :], rhs=QZ[:, h, 32 + 8 * c:40 + 8 * c],
                                start=False, stop=False, skip_group_check=True))
                    T.op('pe', fns, reads=[('attm', bi), ('vTM', us), 'QZ', 'Ssb'], writes=[bk(bo)])
                    T.op('act', lambda e: e.activation(out=oTs[:, :, :], in_=banks[bo][:, 0:384].rearrange("p (h n) -> p h n", h=4),
                                                       func=AF.Copy), reads=[bk(bo)], writes=[('sgt', 1)])
                    T.op('pe', [(lambda e, h=h: e.matmul(banks[b3][0:96, h * 128:(h + 1) * 128], lhsT=oTs[:, h, :],
                                                         rhs=cs[:, C_ID:C_ID + 128], start=True, stop=True)) for h in range(4)],
                         reads=[('sgt', 1), 'cs'], writes=[bk(b3)])
                if not isS:
                    state_update()
                T.op('act', lambda e: e.activation(out=sgt[0][0:r, 0:512], in_=banks[b3][0:r, :], func=AF.Square), reads=[bk(b3)], writes=[('sgt', 0)])
                T.op('dve', lambda e: e.tensor_reduce(out=gst[0:r, 0:4], in_=sgt[0][0:r, 0:512].rearrange("p (h v) -> p h v", h=4),
                                                      op=ALU.add, axis=AX.X), reads=[('sgt', 0)], writes=[('gst0', bi)])
                T.op('act', lambda e: e.activation(out=gst[0:r, 4:8], in_=gst[0:r, 0:4], func=AF.Ln, scale=1.0 / DV, bias=EPS),
                     reads=[('gst0', bi)], writes=[('gst1', bi)])
                T.op('act', lambda e: e.activation(out=gst[0:r, 0:4], in_=gst[0:r, 4:8], func=AF.Exp, scale=-0.5),
                     reads=[('gst1', bi)], writes=[('gst0', bi)])
                for h in range(4):
                    T.op('dve', lambda e, h=h: e.scalar_tensor_tensor(
                        out=ofb[0:r, h * 128:(h + 1) * 128], in0=banks[b3][0:r, h * 128:(h + 1) * 128], scalar=gst[0:r, h:h + 1],
                        in1=G2[0:r, us, h * 128:(h + 1) * 128], op0=ALU.mult, op1=ALU.mult),
                        reads=[bk(b3), ('gst0', bi), ('G2', us)], writes=[('ofb', h)])
                yield
                b5 = nbank()
                T.op('pe', [(lambda e, h=h: e.transpose(out=banks_b[b5][:, h * 128:h * 128 + r], in_=ofb[0:r, h * 128:(h + 1) * 128],
                                                        identity=ident_b[0:r, 0:r])) for h in range(4)],
                     reads=[('ofb', 0), ('ofb', 1), ('ofb', 2), ('ofb', 3), 'identb'], writes=[bk(b5)])
                T.op('act', lambda e: e.activation(out=ofT[:, :, c0:c0 + r],
                                                   in_=banks_b[b5][:, 0:512].rearrange("p (h n) -> p h n", h=4)[:, :, 0:r], func=AF.Copy),
                     reads=[bk(b5)], writes=[('ofT', uslot[u])])
                if isS:
                    for h in range(4):
                        for ee in range(2):
                            src = eT if ee == (h % 2) else eSW
                            skey = ('eT', bi) if ee == (h % 2) else 'eSW'
                            T.op('dve', lambda e, h=h, ee=ee, src=src: e.tensor_copy(
                                out=decS[ee * 64:ee * 64 + 64, :, h], in_=src[ee * 64:ee * 64 + 64, h // 2, 35 + 4 * ee:96:8]),
                                reads=[skey], writes=['decS'])
                    def sin_load(c):
                        i2 = c % 2
                        T.dma('sp', f'd_sin{i2}', Sin[i2][0:64, :, :], sgla[2 * c].rearrange("h k v -> k h v"), writes=[('Sin', i2)])
                        T.dma('sp', f'd_sin{i2}', Sin[i2][64:128, :, :], sgla[2 * c + 1].rearrange("h k v -> k h v"), writes=[('Sin', i2)])

                    def sample_prep(c):
                        i2 = c % 2
                        for ee in range(2):
                            s_ = 2 * c + ee
                            T.op('dve', lambda e, ee=ee, s_=s_: e.tensor_scalar(out=kdblk[0:96, i2, :, ee, :], in0=kdecS[0:96, :].rearrange("p (h k) -> p h k", h=4),
                                                                                scalar1=cs[0:96, C_SEG + s_:C_SEG + s_ + 1], scalar2=None, op0=ALU.mult),
                                 reads=['kdecS', 'cs'], writes=[('kdblk', i2)])

                    def sample_pair(c, us=us):
                        bb = nbank()
                        i2 = c % 2
                        if c == 0:
                            sin_load(0)
                            sample_prep(0)
                        if c + 1 < 8:
                            sin_load(c + 1)
                        T.op('pe', [(lambda e, h=h: e.matmul(banks[bb][:, h * 128:(h + 1) * 128],
                                                             lhsT=kdblk[0:96, i2, h, :, :].rearrange("p e k -> p (e k)"),
                                                             rhs=vTM[0:96, us, h * 128:(h + 1) * 128], start=True, stop=True)) for h in range(4)],
                             reads=[('kdblk', i2), ('vTM', us)], writes=[bk(bb)])
                        for h in range(4):
                            T.op('dve', lambda e, h=h: e.scalar_tensor_tensor(
                                out=Sin[i2][:, h, :], in0=Sin[i2][:, h, :], scalar=decS[:, c, h:h + 1],
                                in1=banks[bb][:, h * 128:(h + 1) * 128], op0=ALU.mult, op1=ALU.add),
                                reads=[('Sin', i2), 'decS', bk(bb)], writes=[('Sin', i2)])
                        T.dma('sp', f'd_so{i2}', gss[2 * c].rearrange("h k v -> k h v"), Sin[i2][0:64, :, :], reads=[('Sin', i2)])
                        T.dma('sp', f'd_so{i2}', gss[2 * c + 1].rearrange("h k v -> k h v"), Sin[i2][64:128, :, :], reads=[('Sin', i2)])
                        if c + 1 < 8:
                            sample_prep(c + 1)
                    for c in range(8):
                        deferred.append(lambda c=c: sample_pair(c))
                if DEBUG_STOP == 'glaS' and isS:
                    T.wait_all('sp', [s_ for s_ in T.sems if s_.startswith('d_')])
                    T.wait_all('sp', ['E_act', 'E_dve', 'E_pe'])
                    es.close()
                    return nc
            spg_slots = {}

            def phaseA():
                for half in range(2):
                    j_ga, sga = next_slab()
                    j_pg, spg = next_slab()
                    for blk in blocks:
                        kind, c0, cn, bu = blk
                        hk = [('hT', uslot[u]) for u in bu]
                        pk = [('pool_out', kind, g) for g in range(4)]
                        for j in range(4):
                            oc = half * 4 + j
                            ba = nbank()
                            mi = 1
                            T.op('pe', [(lambda e, k=k: e.matmul(banks[ba][:, 0:cn], lhsT=ring[sga][:, k, j * 128:(j + 1) * 128],
                                                                 rhs=hT[:, k, c0:c0 + cn], start=(k == 0), stop=(k == 7))) for k in range(8)],
                                 reads=[('ring', sga)] + hk, writes=[bk(ba)])
                            yield
                            T.op('act', lambda e: e.activation(out=sgt[mi][:, 0:cn], in_=banks[ba][:, 0:cn], func=AF.Exp, scale=-1.0),
                                 reads=[bk(ba)], writes=[('sgt', mi)])
                            T.op('act', lambda e: e.activation(out=sgt[mi][:, 0:cn], in_=sgt[mi][:, 0:cn], func=AF.Ln, bias=1.0),
                                 reads=[('sgt', mi)], writes=[('sgt', mi)])
                            T.op('act', lambda e: e.activation(out=sgt[mi][:, 0:cn], in_=sgt[mi][:, 0:cn], func=AF.Exp, scale=-1.0),
                                 reads=[('sgt', mi)], writes=[('sgt', mi)])
                            bya = nbank()
                            T.op('pe', [(lambda e, k=k: e.matmul(banks[bya][:, 0:cn], lhsT=ring[spg][:, k, j * 128:(j + 1) * 128],
                                                                 rhs=pool_out[:, k, c0:c0 + cn], start=(k == 0), stop=(k == 3))) for k in range(4)],
                                 reads=[('ring', spg)] + pk, writes=[bk(bya)])
                            T.op('dve', lambda e: e.tensor_tensor(out=merged[:, oc, c0:c0 + cn], in0=banks[bya][:, 0:cn], in1=sgt[mi][:, 0:cn], op=ALU.mult),
                                 reads=[bk(bya), ('sgt', mi)], writes=[('merged', kind, oc)])
                            yield
                    slab_release(j_ga)
                    slab_release(j_pg)

            fill = phaseA()
            T.tag = f'p{pi}_gla'
            gens = [gla_unit(u, i_ % 2) for i_, u in enumerate(order)]
            nU = len(gens)

            def adv(t):
                if 0 <= t < nU:
                    T.tag = f'p{pi}_gla'
                    next(gens[t], None)
                    T.tag = f'p{pi}_mrgA'
                    next(fill, None)
            adv(0); adv(0)
            for t in range(nU):
                adv(t + 1)
                adv(t)
                adv(t + 1)
                adv(t)
            T.tag = f'p{pi}_mrgA'
            for _ in fill:
                pass
            if pi == len(passes) - 1:
                T.dma('sp', 'd_out', gsp[0:4:2].rearrange("c k v -> k c v"), Sf[0:64, :, :], reads=[('Sf', 0), ('Sf', 1), ('Sf', 2), ('Sf', 3)])
                T.dma('sp', 'd_out', gsp[1:4:2].rearrange("c k v -> k c v"), Sf[64:128, :, :], reads=[('Sf', 0), ('Sf', 1), ('Sf', 2), ('Sf', 3)])

            T.tag = f'p{pi}_mrgB'
            for half in range(2):
                j_gb, sgb = next_slab()
                j_pg, spg = next_slab()
                for blk in blocks:
                    kind, c0, cn, bu = blk
                    hk = [('hT', uslot[u]) for u in bu]
                    okk = [('ofT', uslot[u]) for u in bu]
                    for j in range(4):
                        oc = half * 4 + j
                        bb_, byb = nbank(), nbank()
                        mi = st['tmp'] % 2; st['tmp'] += 1
                        T.op('pe', [(lambda e, k=k: e.matmul(banks[bb_][:, 0:cn], lhsT=ring[sgb][:, k, j * 128:(j + 1) * 128],
                                                             rhs=hT[:, k, c0:c0 + cn], start=(k == 0), stop=(k == 7))) for k in range(8)],
                             reads=[('ring', sgb)] + hk, writes=[bk(bb_)])
                        T.op('pe', [(lambda e, k=k: e.matmul(banks[byb][:, 0:cn], lhsT=ring[spg][:, k, j * 128:(j + 1) * 128],
                                                             rhs=ofT[:, k, c0:c0 + cn], start=(k == 0), stop=(k == 3))) for k in range(4)],
                             reads=[('ring', spg)] + okk, writes=[bk(byb)])
                        T.op('act', lambda e: e.activation(out=sgt[mi][:, 0:cn], in_=banks[bb_][:, 0:cn], func=AF.Exp, scale=-1.0),
                             reads=[bk(bb_)], writes=[('sgt', mi)])
                        T.op('act', lambda e: e.activation(out=sgt[mi][:, 0:cn], in_=sgt[mi][:, 0:cn], func=AF.Ln, bias=1.0),
                             reads=[('sgt', mi)], writes=[('sgt', mi)])
                        T.op('act', lambda e: e.activation(out=sgt[mi][:, 0:cn], in_=sgt[mi][:, 0:cn], func=AF.Exp, scale=-1.0),
                             reads=[('sgt', mi)], writes=[('sgt', mi)])
                        T.op('dve', lambda e: e.tensor_tensor(out=sgt[mi][:, 0:cn], in0=banks[byb][:, 0:cn], in1=sgt[mi][:, 0:cn], op=ALU.mult),
                             reads=[bk(byb), ('sgt', mi)], writes=[('sgt', mi)])
                        T.op('dve', lambda e: e.tensor_tensor(out=merged[:, oc, c0:c0 + cn], in0=merged[:, oc, c0:c0 + cn], in1=sgt[mi][:, 0:cn], op=ALU.add),
                             reads=[('sgt', mi), ('merged', kind, oc)], writes=[('merged', kind, oc)])
                slab_release(j_gb)
                slab_release(j_pg)

            j_w0, sw0 = next_slab()
            j_w1, sw1 = next_slab()
            prev_u = None
            for u in order:
                T.tag = f'p{pi}_wout'
                us = uslot[u]; r = rows[u]
                kind = 'S' if u == 'S' else 'P'
                c0, cn = ucol[u]
                for half, slot in ((0, sw0), (1, sw1)):
                    b = nbank()
                    T.op('pe', [(lambda e, k=k: e.matmul(banks[b][0:r, :], lhsT=merged[:, k, c0:c0 + r], rhs=ring[slot][:, k, :],
                                                         start=(k == 0), stop=(k == 7))) for k in range(8)],
                         reads=[('ring', slot)] + [('merged', kind, oc) for oc in range(8)], writes=[bk(b)])
                    T.op('dve', lambda e: e.tensor_tensor(out=xres[0:r, us, half * 512:(half + 1) * 512],
                                                          in0=xres[0:r, us, half * 512:(half + 1) * 512], in1=banks[b][0:r, :], op=ALU.add),
                         reads=[bk(b), ('x', us)], writes=[('x', us)])
                T.tag = f'p{pi}_norm2'
                norm_a(u, gffn_bc, 'gffn', 20 + uslot[u] * 4)
                if prev_u is not None:
                    norm_b(prev_u)
                prev_u = u
            norm_b(prev_u)
            slab_release(j_w0)
            slab_release(j_w1)

            nxt = passes[pi + 1] if pi + 1 < len(passes) else None
            Eb_flat = Eb[:].rearrange("p g c -> p (g c)")
            EKEYS = [('E', g_) for g_ in range(4)] + [('Eh', g_) for g_ in range(4)]
            hbX = actb_flat[:, 6 * MC:6 * MC + 1024]
            HBXK = ['hbX', ('act', 'P', 6), ('act', 'P', 7), ('act', 'S', 6), ('act', 'S', 7)]

            def pf_bufs(idx):
                return (hTM[0][:, :], [('hTM', 0)]) if idx % 2 == 0 else (hbX, HBXK)

            def pf_chain(idx):
                u_ = nxt[idx]; i_ = idx % 2
                stg = Eb_flat[:, i_ * 1024:(i_ + 1) * 1024]
                ssc = 60 + idx * 4
                hb, hkeys = pf_bufs(idx)
                T.dma('sp', f'd_xs{i_}', stg, xp[u_ * 128:(u_ + 1) * 128, :], writes=[('xstg', i_)] + (EKEYS if idx == 0 else []))
                T.op('act', lambda e: e.activation(out=hb, in_=stg, func=AF.Square, accum_out=stat[:, ssc:ssc + 1]),
                     reads=[('xstg', i_)], writes=hkeys + [('stat', ssc)])
                T.op('act', lambda e: e.activation(out=stat[:, ssc + 1:ssc + 2], in_=stat[:, ssc:ssc + 1], func=AF.Ln, scale=1.0 / D, bias=EPS),
                     reads=[('stat', ssc)], writes=[('stat', ssc + 1)])
                T.op('act', lambda e: e.activation(out=stat[:, ssc + 2:ssc + 3], in_=stat[:, ssc + 1:ssc + 2], func=AF.Exp, scale=-0.5),
                     reads=[('stat', ssc + 1)], writes=[('stat', ssc + 2)])
                T.op('dve', lambda e: e.scalar_tensor_tensor(out=hb, in0=stg, scalar=stat[:, ssc + 2:ssc + 3], in1=gmix_bc[:, :],
                                                             op0=ALU.mult, op1=ALU.mult),
                     reads=[('xstg', i_), ('stat', ssc + 2), 'gmix'], writes=hkeys)

            def pf_tr(idx):
                hb, hkeys = pf_bufs(idx)
                b = nbank()
                T.op('pe', [(lambda e, c=c: e.transpose(out=banks_b[b][:, c * 128:(c + 1) * 128], in_=hb[:, c * 128:(c + 1) * 128],
                                                         identity=ident_b[:, :])) for c in range(8)],
                     reads=hkeys + ['identb'], writes=[bk(b)])
                T.op('act', lambda e: e.activation(out=hT[:, :, idx * 128:(idx + 1) * 128],
                                                   in_=banks_b[b][:, :].rearrange("p (c n) -> p c n", c=8), func=AF.Copy),
                     reads=[bk(b)], writes=[('hT', idx)])

            T.tag = f'p{pi}_ffn'
            for gi, (f0, nf) in enumerate(FF_GROUPS):
                do_pf = (nxt is not None) and gi == len(FF_GROUPS) - 1
                if do_pf:
                    T.tag = f'p{pi}_pf'
                    pf_chain(0); pf_chain(1)
                    T.tag = f'p{pi}_ffn'
                for sub in range(0, nf, 4):
                    ns = min(4, nf - sub)
                    fa = f0 + sub
                    j_sg, sg = next_slab()
                    j_su, su = next_slab()
                    s_tails = []
                    for blk in blocks:
                        kind, c0, cn, bu = blk
                        hk = [('hT', uslot[u]) for u in bu]
                        if kind == 'S':
                            nw = ns * 128
                            bg, bu_ = nbank(), nbank()
                            ti = st['tmp'] % 2; st['tmp'] += 1
                            T.op('pe', [(lambda e, k=k: e.matmul(banks[bg][0:SROWS, 0:nw], lhsT=hT[:, k, c0:c0 + SROWS], rhs=ring[sg][:, k, 0:nw],
                                                                 start=(k == 0), stop=(k == 7))) for k in range(8)],
                                 reads=[('ring', sg)] + hk, writes=[bk(bg)])
                            T.op('pe', [(lambda e, k=k: e.matmul(banks[bu_][0:SROWS, 0:nw], lhsT=hT[:, k, c0:c0 + SROWS], rhs=ring[su][:, k, 0:nw],
                                                                 start=(k == 0), stop=(k == 7))) for k in range(8)],
                                 reads=[('ring', su)] + hk, writes=[bk(bu_)])
                            T.op('act', lambda e: e.activation(out=sgt[ti][0:SROWS, 0:nw], in_=banks[bg][0:SROWS, 0:nw], func=AF.Silu),
                                 reads=[bk(bg)], writes=[('sgt', ti)])
                            okeys = [('ofb', h_) for h_ in range(4)]
                            T.op('dve', lambda e: e.tensor_tensor(out=ofb[0:SROWS, 0:nw], in0=banks[bu_][0:SROWS, 0:nw],
                                                                  in1=sgt[ti][0:SROWS, 0:nw], op=ALU.mult),
                                 reads=[bk(bu_), ('sgt', ti)], writes=okeys)
                            def s_tail(c0=c0, nw=nw, okeys=okeys, kind=kind):
                                bt = nbank()
                                T.op('pe', [(lambda e, j=j: e.transpose(out=banks_b[bt][:, j * 128:j * 128 + SROWS], in_=ofb[0:SROWS, j * 128:(j + 1) * 128],
                                                                        identity=ident_b[0:SROWS, 0:SROWS])) for j in range(ns)],
                                     reads=okeys + ['identb'], writes=[bk(bt)])
                                T.op('act', lambda e: e.activation(
                                    out=actb[:, sub:sub + ns, c0:c0 + SROWS],
                                    in_=banks_b[bt][:, 0:nw].rearrange("p (j n) -> p j n", j=ns)[:, :, 0:SROWS], func=AF.Copy),
                                    reads=[bk(bt)], writes=[('act', kind, sub + j) for j in range(ns)] + HBK)
                            s_tails.append(s_tail)
                            continue
                        for j in range(ns):
                            bg, bu_ = nbank(), nbank()
                            ti = st['tmp'] % 2; st['tmp'] += 1
                            T.op('pe', [(lambda e, k=k: e.matmul(banks[bg][:, 0:cn], lhsT=ring[sg][:, k, j * 128:(j + 1) * 128],
                                                                 rhs=hT[:, k, c0:c0 + cn], start=(k == 0), stop=(k == 7))) for k in range(8)],
                                 reads=[('ring', sg)] + hk, writes=[bk(bg)])
                            T.op('pe', [(lambda e, k=k: e.matmul(banks[bu_][:, 0:cn], lhsT=ring[su][:, k, j * 128:(j + 1) * 128],
                                                                 rhs=hT[:, k, c0:c0 + cn], start=(k == 0), stop=(k == 7))) for k in range(8)],
                                 reads=[('ring', su)] + hk, writes=[bk(bu_)])
                            T.op('act', lambda e: e.activation(out=sgt[ti][:, 0:cn], in_=banks[bg][:, 0:cn], func=AF.Silu),
                                 reads=[bk(bg)], writes=[('sgt', ti)])
                            T.op('dve', lambda e: e.tensor_tensor(out=actb[:, sub + j, c0:c0 + cn], in0=banks[bu_][:, 0:cn],
                                                                  in1=sgt[ti][:, 0:cn], op=ALU.mult),
                                 reads=[bk(bu_), ('sgt', ti)], writes=[('act', kind, sub + j)] + HBK)
                    for f_ in s_tails:
                        f_()
                    slab_release(j_sg)
                    slab_release(j_su)
                    if deferred:
                        T.tag = f'p{pi}_smpl'
                        deferred.pop(0)()
                        T.tag = f'p{pi}_ffn'
                if do_pf:
                    T.tag = f'p{pi}_pf'
                    pf_tr(0); pf_chain(2); pf_tr(1); pf_chain(3)
                    T.tag = f'p{pi}_ffn'
                for half in range(2):
                    if do_pf and half == 1:
                        T.tag = f'p{pi}_pf'
                        pf_tr(2); pf_tr(3)
                        T.tag = f'p{pi}_ffn'
                    j_fo, slot = next_slab()
                    for u in order:
                        us = uslot[u]; r = rows[u]
                        kind = 'S' if u == 'S' else 'P'
                        c0, cn = ucol[u]
                        b = nbank()
                        T.op('pe', [(lambda e, k=k: e.matmul(banks[b][0:r, :], lhsT=actb[:, k, c0:c0 + r], rhs=ring[slot][:, k, :],
                                                             start=(k == 0), stop=(k == nf - 1))) for k in range(nf)],
                             reads=[('ring', slot)] + [('act', kind, k) for k in range(nf)] + HBK, writes=[bk(b)])
                        T.op('dve', lambda e: e.tensor_tensor(out=xres[0:r, us, half * 512:(half + 1) * 512],
                                                              in0=xres[0:r, us, half * 512:(half + 1) * 512], in1=banks[b][0:r, :], op=ALU.add),
                             reads=[bk(b), ('x', us)], writes=[('x', us)])
                    slab_release(j_fo)
                    if deferred:
                        T.tag = f'p{pi}_smpl'
                        deferred.pop(0)()
                        T.tag = f'p{pi}_ffn'
            while deferred:
                deferred.pop(0)()
            T.tag = f'p{pi}_final'
            for u in order:
                us = uslot[u]; r = rows[u]
                ssc = 40 + us * 4
                xt = xres[0:r, us, :]
                T.op('act', lambda e: e.activation(out=hTM[0][0:r, :], in_=xt, func=AF.Square, accum_out=stat[0:r, ssc:ssc + 1]),
                     reads=[('x', us)], writes=[('hTM', 0), ('stat', ssc)])
                T.op('act', lambda e: e.activation(out=stat[0:r, ssc + 1:ssc + 2], in_=stat[0:r, ssc:ssc + 1], func=AF.Ln,
                                                   scale=1.0 / D, bias=EPS),
                     reads=[('stat', ssc)], writes=[('stat', ssc + 1)])
                T.op('act', lambda e: e.activation(out=stat[0:r, ssc + 2:ssc + 3], in_=stat[0:r, ssc + 1:ssc + 2], func=AF.Exp,
                                                   scale=-0.5),
                     reads=[('stat', ssc + 1)], writes=[('stat', ssc + 2)])
                T.op('dve', lambda e: e.scalar_tensor_tensor(out=xt, in0=xt, scalar=stat[0:r, ssc + 2:ssc + 3],
                                                             in1=gfin_bc[0:r, :], op0=ALU.mult, op1=ALU.mult),
                     reads=[('x', us), ('stat', ssc + 2), 'gfin'], writes=[('x', us)])
                if u == 'S':
                    T.dma('sp', f'd_y{us}', y_s[:, :], xres[32:96, us, :], reads=[('x', us)])
                else:
                    T.dma('sp', f'd_y{us}', y_p[u * 128:(u + 1) * 128, :], xt, reads=[('x', us)])
            prefetched = nxt is not None

    except StopBuild:
        pass
    T.wait_all('sp', [s for s in T.sems if s.startswith('d_')])
    if OP_LIMIT is not None:
        T.wait_all('sp', ['E_pe', 'E_act', 'E_dve'])
    es.close()
    global LAST_LOG
    LAST_LOG = T.log
    return nc


LAST_LOG = None
_CONSTS = None


def kernel(x_prompt, x_sample, state_pool, state_gla, meta_tokens, g_mix, w_in, w_gk_up, b_gk,
           w_pool_group, pool_scale, w_pool_proj, g_gla_norm, w_gla_proj, w_out, g_ffn,
           w_ffn_in, w_ffn_out, g_final):
    global _CONSTS
    if _CONSTS is None:
        _CONSTS = _build_consts()
    f = lambda a: np.ascontiguousarray(np.asarray(a, dtype=np.float32))
    x_prompt = f(x_prompt); x_sample = f(x_sample); state_pool = f(state_pool); state_gla = f(state_gla)
    shared = {
        "meta": f(meta_tokens), "g_mix": f(g_mix).reshape(1, D), "w_in": f(w_in)[0], "w_gk_up": f(w_gk_up)[0],
        "b_gk": f(b_gk).reshape(1, 256), "w_pool_group": f(w_pool_group)[0], "pool_scale": f(pool_scale).reshape(512, 1),
        "w_pool_proj": f(w_pool_proj)[0], "g_gla_norm": f(g_gla_norm).reshape(1, 128), "w_gla_proj": f(w_gla_proj)[0],
        "w_out": f(w_out)[0], "g_ffn": f(g_ffn).reshape(1, D), "w_ffn_in": f(w_ffn_in)[0], "w_ffn_out": f(w_ffn_out)[0],
        "g_final": f(g_final).reshape(1, D), "consts": _CONSTS,
    }
    in_maps = []
    for c in range(8):
        m = dict(shared)
        m["xp"] = x_prompt[c]
        m["xs"] = x_sample[16 * c:16 * (c + 1)].reshape(64, D)
        m["spool"] = state_pool[0, 16 * c:16 * (c + 1)].reshape(240, 512)
        m["sgla"] = state_gla[0, 16 * c:16 * (c + 1)]
        in_maps.append(m)
    nc = build_program()
    res = run_bass_kernel_spmd(nc, in_maps, core_ids=list(range(8)))
    R = res.results
    y_prompt = np.stack([R[c]["y_p"] for c in range(8)], 0).astype(np.float32)
    y_sample = np.concatenate([R[c]["y_s"].reshape(16, 4, D) for c in range(8)], 0).astype(np.float32)
    pool_p = np.stack([R[c]["pbp"] for c in range(8)], 0)[None].astype(np.float32)
    gla_p = np.stack([R[c]["gsp"] for c in range(8)], 0)[None].astype(np.float32)
    pool_s = np.concatenate([R[c]["pbs"] for c in range(8)], 0)[None].astype(np.float32)
    gla_s = np.concatenate([R[c]["gss"] for c in range(8)], 0)[None].astype(np.float32)
    return (y_prompt, y_sample, pool_p, gla_p, pool_s, gla_s)
```
